# Optimizing a Trainium2 kernel written in Bass

```python
import jax, jax.numpy as jnp
from jax import lax
import numpy as np

D_MODEL = 1024
BATCH = 16
SEQ = 2048
DEPTH = 2

PLE_DIM = 256
CHUNK = 128
A_HEADS = 4
A_HEAD_DIM = 128
A_WIDTH = A_HEADS * A_HEAD_DIM
B_HEADS = 8
B_HEAD_DIM = 64
B_WIDTH = B_HEADS * B_HEAD_DIM
MIX_WIDTH = A_WIDTH + B_WIDTH
IN_WIDTH = 2 * A_WIDTH + 3 * B_WIDTH
B_CONV = 3
C_WIDTH = D_MODEL
C_CONV = 31
D_FF = 2816
FFN_CONV = 3
N_EVEN = (DEPTH + 1) // 2
N_ODD = DEPTH // 2
EPS = 1e-6

kernel_name = "hybrid_gmlp_shortconv_conformer_block"


def rmsnorm(x, g):
    xf = x.astype(jnp.float32)
    y = xf * lax.rsqrt(jnp.mean(xf * xf, axis=-1, keepdims=True) + EPS)
    return (y * g.astype(jnp.float32)).astype(x.dtype)


def layernorm(x, g, b):
    xf = x.astype(jnp.float32)
    mu = jnp.mean(xf, axis=-1, keepdims=True)
    xc = xf - mu
    var = jnp.mean(xc * xc, axis=-1, keepdims=True)
    y = xc * lax.rsqrt(var + EPS) * g.astype(jnp.float32) + b.astype(jnp.float32)
    return y.astype(x.dtype)


def causal_dwconv(x, w):
    K, C = w.shape
    return lax.conv_general_dilated(
        x, w[:, None, :].astype(x.dtype), window_strides=(1,), padding=[(K - 1, 0)],
        dimension_numbers=('NWC', 'WIO', 'NWC'), feature_group_count=C)


def gmlp_spatial_gate(u, v, w_s, b_s, ln_g, ln_b):
    bsz, s, _ = u.shape
    n = s // CHUNK
    vh = v.reshape(bsz, n, CHUNK, A_HEADS, A_HEAD_DIM)
    vh = layernorm(vh, ln_g.reshape(A_HEADS, A_HEAD_DIM), ln_b.reshape(A_HEADS, A_HEAD_DIM))
    mask = jnp.tril(jnp.ones((CHUNK, CHUNK), dtype=bool))
    w = jnp.where(mask[None], w_s, jnp.zeros_like(w_s)).astype(v.dtype)
    mixed = jnp.einsum('hts,bnshd->bnthd', w, vh) + b_s.T.astype(v.dtype)[None, None, :, :, None]
    return u * mixed.reshape(bsz, s, A_WIDTH)


def even_mixer(h, w_in, a_ws, a_bs, a_ln_g, a_ln_b, b_conv_w, w_out):
    z = h @ w_in
    a_u, a_v, b_b, b_c, b_h = jnp.split(
        z, [A_WIDTH, 2 * A_WIDTH, 2 * A_WIDTH + B_WIDTH, 2 * A_WIDTH + 2 * B_WIDTH], axis=-1)
    a_out = gmlp_spatial_gate(jax.nn.gelu(a_u), jax.nn.gelu(a_v), a_ws, a_bs, a_ln_g, a_ln_b)
    b_out = b_b * causal_dwconv(b_c * b_h, b_conv_w)
    return jnp.concatenate([a_out, b_out], axis=-1) @ w_out


def conformer_conv(h, w_in, b_in, dw_w, dw_b, ln_g, ln_b, w_out, b_out):
    z = h @ w_in + b_in
    a, g = jnp.split(z, 2, axis=-1)
    y = a * jax.nn.sigmoid(g)
    y = causal_dwconv(y, dw_w) + dw_b
    y = jax.nn.silu(layernorm(y, ln_g, ln_b))
    return y @ w_out + b_out


def conv_ffn(h, w_up, dw_w, dw_b, w_down):
    z = causal_dwconv(h @ w_up, dw_w) + dw_b
    g, u = jnp.split(z, 2, axis=-1)
    return (jax.nn.silu(g) * u) @ w_down


def setup_inputs(seed: int = 0) -> dict:
    key = jax.random.key(seed)
    ks = iter(jax.random.split(key, 64))

    def nrm(shape, scale):
        return jax.random.normal(next(ks), shape, jnp.float32) * scale

    def gain(shape):
        return 1.0 + nrm(shape, 0.02)

    d = D_MODEL
    return {
        "x": nrm((BATCH, SEQ, d), 1.0),
        "p": nrm((DEPTH, BATCH, SEQ, PLE_DIM), 1.0),
        "ev_norm": gain((N_EVEN, d)),
        "ev_w_in": nrm((N_EVEN, d, IN_WIDTH), d ** -0.5),
        "ev_a_ws": nrm((N_EVEN, A_HEADS, CHUNK, CHUNK), CHUNK ** -0.5),
        "ev_a_bs": gain((N_EVEN, A_HEADS, CHUNK)),
        "ev_a_ln_g": gain((N_EVEN, A_WIDTH)),
        "ev_a_ln_b": nrm((N_EVEN, A_WIDTH), 0.02),
        "ev_b_conv_w": nrm((N_EVEN, B_CONV, B_WIDTH), B_CONV ** -0.5),
        "ev_w_out": nrm((N_EVEN, MIX_WIDTH, d), MIX_WIDTH ** -0.5),
        "od_norm": gain((N_ODD, d)),
        "od_w_in": nrm((N_ODD, d, 2 * C_WIDTH), d ** -0.5),
        "od_b_in": nrm((N_ODD, 2 * C_WIDTH), 0.02),
        "od_dw_w": nrm((N_ODD, C_CONV, C_WIDTH), C_CONV ** -0.5),
        "od_dw_b": nrm((N_ODD, C_WIDTH), 0.02),
        "od_ln_g": gain((N_ODD, C_WIDTH)),
        "od_ln_b": nrm((N_ODD, C_WIDTH), 0.02),
        "od_w_out": nrm((N_ODD, C_WIDTH, d), C_WIDTH ** -0.5),
        "od_b_out": nrm((N_ODD, d), 0.02),
        "ffn_norm": gain((DEPTH, d)),
        "ffn_w_up": nrm((DEPTH, d, 2 * D_FF), d ** -0.5),
        "ffn_dw_w": nrm((DEPTH, FFN_CONV, 2 * D_FF), FFN_CONV ** -0.5),
        "ffn_dw_b": nrm((DEPTH, 2 * D_FF), 0.02),
        "ffn_w_down": nrm((DEPTH, D_FF, d), D_FF ** -0.5),
        "ple_w_p": nrm((DEPTH, PLE_DIM, d), PLE_DIM ** -0.5),
        "ple_norm": gain((DEPTH, d)),
        "ple_w_g": nrm((DEPTH, d, d), d ** -0.5),
        "final_norm": gain((d,)),
    }


def reference(x, p, ev_norm, ev_w_in, ev_a_ws, ev_a_bs, ev_a_ln_g, ev_a_ln_b, ev_b_conv_w, ev_w_out,
              od_norm, od_w_in, od_b_in, od_dw_w, od_dw_b, od_ln_g, od_ln_b, od_w_out, od_b_out,
              ffn_norm, ffn_w_up, ffn_dw_w, ffn_dw_b, ffn_w_down, ple_w_p, ple_norm, ple_w_g,
              final_norm):
    r = x
    for i in range(DEPTH):
        j = i // 2
        if i % 2 == 0:
            h = rmsnorm(r, ev_norm[j])
            r = r + even_mixer(h, ev_w_in[j], ev_a_ws[j], ev_a_bs[j], ev_a_ln_g[j], ev_a_ln_b[j],
                               ev_b_conv_w[j], ev_w_out[j])
        else:
            h = rmsnorm(r, od_norm[j])
            r = r + conformer_conv(h, od_w_in[j], od_b_in[j], od_dw_w[j], od_dw_b[j],
                                   od_ln_g[j], od_ln_b[j], od_w_out[j], od_b_out[j])
        h = rmsnorm(r, ffn_norm[i])
        r = r + conv_ffn(h, ffn_w_up[i], ffn_dw_w[i], ffn_dw_b[i], ffn_w_down[i])
        gate = jax.nn.sigmoid(rmsnorm(r, ple_norm[i]) @ ple_w_g[i])
        r = r + gate * (p[i] @ ple_w_p[i])
    return rmsnorm(r, final_norm)
```

```python
import numpy as np
import concourse.bass as bass
import concourse.mybir as mybir
from concourse.bass_utils import run_bass_kernel_spmd

F32 = mybir.dt.float32
BF16 = mybir.dt.bfloat16
AF = mybir.ActivationFunctionType
ALU = mybir.AluOpType
AX = mybir.AxisListType

NCORE = 8
S = 2048
NT = 4
TT = 512
KC = 8
D = 1024
DFF = 2816
NSEQ = 2
EPS = 1e-6
NSLOT = 5
WDEPTH = 3
NDIAG = 32

ENGS = ("pe", "act", "dve", "pool", "sp")


class Res:
    __slots__ = ("name", "w", "rs", "psum")

    def __init__(self, name="", psum=False):
        self.name = name
        self.w = None
        self.rs = []
        self.psum = psum


class Op:
    __slots__ = ("eng", "idx", "fn", "deps", "need_sig", "sig", "dma", "dsem", "dval")

    def __init__(self, eng, idx, fn, dma):
        self.eng = eng
        self.idx = idx
        self.fn = fn
        self.deps = []
        self.need_sig = False
        self.sig = 0
        self.dma = dma
        self.dsem = -1
        self.dval = 0


class Prog:
    def __init__(self, n_dma_sems=12):
        self.ops = {e: [] for e in ENGS}
        self.n_dma_sems = n_dma_sems
        self.dma_count = {e: 0 for e in ENGS}
        self.dma_last = {}
        self.out_dmas = []

    def op(self, eng, fn, reads=(), writes=(), dma=False):
        o = Op(eng, len(self.ops[eng]), fn, dma)
        deps = {}
        for r in reads:
            if r.w is not None:
                deps[id(r.w)] = r.w
            if r.psum:
                for rd in r.rs:
                    if rd.eng != eng:
                        deps[id(rd)] = rd
        for w in writes:
            if w.w is not None:
                deps[id(w.w)] = w.w
            for rd in w.rs:
                deps[id(rd)] = rd
        if dma:
            k = self.dma_count[eng]
            self.dma_count[eng] += 1
            slot = k % self.n_dma_sems
            o.dsem = slot
            o.dval = 16 * (k // self.n_dma_sems + 1)
            prev = self.dma_last.get((eng, slot))
            if prev is not None:
                deps[id(prev)] = prev
            self.dma_last[(eng, slot)] = o
        for d in deps.values():
            if d is o:
                continue
            if (not d.dma) and d.eng == "pe" and eng == "pe" and not dma:
                continue
            o.deps.append(d)
            if not d.dma:
                d.need_sig = True
        for r in reads:
            if not dma:
                r.rs = [x for x in r.rs if x.dma or x.eng != eng]
            r.rs.append(o)
        for w in writes:
            w.w = o
            w.rs = []
        self.ops[eng].append(o)
        return o

    def emit(self, nc):
        for e in ENGS:
            n = 0
            for o in self.ops[e]:
                if o.need_sig:
                    n += 1
                    o.sig = n
        esem = {e: nc.alloc_semaphore(f"es_{e}") for e in ENGS}
        dsem = {}
        for e in ENGS:
            if self.dma_count[e]:
                dsem[e] = [nc.alloc_semaphore(f"ds_{e}_{i}") for i in range(self.n_dma_sems)]
        handles = {"pe": "tensor", "act": "scalar", "dve": "vector", "pool": "gpsimd", "sp": "sync"}
        prog = self
        all_sems = list(esem.values()) + [x for v in dsem.values() for x in v]
        for sm in all_sems:
            nc.gpsimd.sem_clear(sm)
        nc.all_engine_barrier()

        def run_engine(e, h):
            waited = {}
            for o in prog.ops[e]:
                need = {}
                for d in o.deps:
                    if d.dma:
                        key = ("d", d.eng, d.dsem)
                        val = d.dval
                    else:
                        key = ("e", d.eng)
                        val = d.sig
                    if waited.get(key, 0) >= val:
                        continue
                    if need.get(key, 0) < val:
                        need[key] = val
                for key, val in need.items():
                    waited[key] = val
                    sem = dsem[key[1]][key[2]] if key[0] == "d" else esem[key[1]]
                    h.wait_ge(sem, val)
                ins = o.fn(h)
                if o.dma:
                    ins.then_inc(dsem[e][o.dsem], 16)
                elif o.need_sig:
                    ins.then_inc(esem[e], 1)
            for o in prog.out_dmas:
                if o.eng == e and waited.get(("d", e, o.dsem), 0) < o.dval:
                    waited[("d", e, o.dsem)] = o.dval
                    h.wait_ge(dsem[e][o.dsem], o.dval)

        with nc.Block() as block:
            for e in ENGS:
                if not prog.ops[e]:
                    continue
                getattr(block, handles[e])(lambda h, e=e: run_engine(e, h))
        for sm in all_sems:
            nc.gpsimd.sem_clear(sm)
        nc.all_engine_barrier()


def carry_of(res_list):
    ops = {}
    for r in res_list:
        if r.w is not None:
            ops[id(r.w)] = r.w
        for x in r.rs:
            ops[id(x)] = x
    best = {}
    out = []
    for o in ops.values():
        if o.dma:
            out.append(o)
        else:
            b = best.get(o.eng)
            if b is None or o.idx > b.idx:
                best[o.eng] = o
    return out + list(best.values())


VEC_LAYOUT = {}
_c = 0
for _n, _w in [("ev_norm", 8), ("od_norm", 8), ("ffn_norm0", 8), ("ffn_norm1", 8),
               ("ple_norm0", 8), ("ple_norm1", 8), ("final_norm", 8),
               ("od_b_in", 16), ("od_dw_b", 8), ("od_ln_g", 8), ("od_ln_b", 8), ("od_b_out", 8),
               ("od_dw_w", 248), ("ffn_dw_w0", 132), ("ffn_dw_w1", 132),
               ("ffn_dw_b0", 44), ("ffn_dw_b1", 44), ("ev_b_conv_w", 12),
               ("ev_a_ln_g", 4), ("ev_a_ln_b", 4)]:
    VEC_LAYOUT[_n] = _c
    _c += _w
NVEC = _c


def _chan(v):
    v = np.asarray(v, np.float32).reshape(-1)
    return v.reshape(-1, 128).T


def _taps(w):
    w = np.asarray(w, np.float32)
    K, C = w.shape
    return w.reshape(K, C // 128, 128).transpose(2, 1, 0).reshape(128, (C // 128) * K)


def pack_vecs(inp):
    v = np.zeros((128, NVEC), np.float32)

    def put(name, arr):
        c = VEC_LAYOUT[name]
        v[:, c:c + arr.shape[1]] = arr

    put("ev_norm", _chan(inp["ev_norm"][0]))
    put("od_norm", _chan(inp["od_norm"][0]))
    for i in range(2):
        put(f"ffn_norm{i}", _chan(inp["ffn_norm"][i]))
        put(f"ple_norm{i}", _chan(inp["ple_norm"][i]))
        tw = _taps(inp["ffn_dw_w"][i]).reshape(128, 2, 22, 3)
        put(f"ffn_dw_w{i}", np.ascontiguousarray(tw.transpose(0, 2, 1, 3)).reshape(128, 132))
        put(f"ffn_dw_b{i}", _chan(inp["ffn_dw_b"][i]))
    put("final_norm", _chan(inp["final_norm"]))
    put("od_b_in", _chan(inp["od_b_in"][0]))
    put("od_dw_b", _chan(inp["od_dw_b"][0]))
    put("od_ln_g", _chan(inp["od_ln_g"][0]))
    put("od_ln_b", _chan(inp["od_ln_b"][0]))
    put("od_b_out", _chan(inp["od_b_out"][0]))
    put("od_dw_w", _taps(inp["od_dw_w"][0]))
    put("ev_b_conv_w", _taps(inp["ev_b_conv_w"][0]))
    put("ev_a_ln_g", _chan(inp["ev_a_ln_g"][0]))
    put("ev_a_ln_b", _chan(inp["ev_a_ln_b"][0]))
    return v


class Builder:
    def __init__(self, nstages=7, nseq=NSEQ):
        self.nstages = nstages
        self.nseq = nseq
        nc = bass.Bass("TRN2", target_bir_lowering=False)
        self.nc = nc

        def din(name, shape):
            return nc.dram_tensor(name, list(shape), F32, kind="ExternalInput").ap()

        self.xT = din("xT", [D, NSEQ * S])
        self.pT = din("pT", [2, 256, NSEQ * S])
        self.w_ev_in = din("w_ev_in", [D, 2560])
        self.w_ev_out = din("w_ev_out", [D, D])
        self.w_od_in = din("w_od_in", [D, 2048])
        self.w_od_out = din("w_od_out", [D, D])
        self.w_up = din("w_up", [2, D, 2 * DFF])
        self.w_down = din("w_down", [2, DFF, D])
        self.w_pp = din("w_pp", [2, 256, D])
        self.w_pg = din("w_pg", [2, D, D])
        self.vecs_d = din("vecs", [128, NVEC])
        self.wsT_d = din("wsT", [128, 512])
        self.bsbc_d = din("bsbc", [128, 512])
        self.oT = nc.dram_tensor("oT", [D, NSEQ * S], F32, kind="ExternalOutput").ap()

        self.r = nc.alloc_sbuf_tensor("r", [128, KC, S], F32)
        self.h = nc.alloc_sbuf_tensor("h", [128, KC, S], BF16)
        self.scr = nc.alloc_sbuf_tensor("scr", [128, 24576], BF16)
        self.wring = nc.alloc_sbuf_tensor("wring", [128, NSLOT, 2048], BF16)
        self.multi = nc.alloc_sbuf_tensor("multi", [128, 4160], BF16)
        self.vecs = nc.alloc_sbuf_tensor("vecs_sb", [128, NVEC], F32)
        self.ident = nc.alloc_sbuf_tensor("ident", [128, 128], BF16)
        self.identf = nc.alloc_sbuf_tensor("identf", [128, 128], F32)
        self.ones_m = nc.alloc_sbuf_tensor("ones_m", [128, 128], BF16)
        self.epsT = nc.alloc_sbuf_tensor("epsT", [128, 1], F32)
        self.dring = nc.alloc_sbuf_tensor("dring", [128, NDIAG, 128], BF16)
        self.tmp = nc.alloc_sbuf_tensor("tmp", [128, 8, 512], F32)
        self.Q = nc.alloc_sbuf_tensor("Q", [128, 4, 128], F32)
        self.wmT = nc.alloc_sbuf_tensor("wmT", [128, 4, 128], BF16)
        self.stat = nc.alloc_sbuf_tensor("stat", [128, 4, 16], F32)
        self.pst = nc.alloc_psum_tensor("pst", [128, 8, 512], F32)
        self.psb = [self.pst[:, i, :] for i in range(8)]

    def reset(self, plan):
        self.P = Prog()
        self.plan_in = plan
        self.plan_out = []
        self.wi = 0
        self.wissued = 0
        self.wcur = [-1] * NSLOT
        self.Rslot = [Res(f"ws{i}") for i in range(NSLOT)]
        self.psr = [Res(f"ps{i}", psum=True) for i in range(8)]
        self.psi = 0
        self.tmr = [Res(f"tm{i}") for i in range(8)]
        self.tmi = 0
        self.tm4 = 0
        self.tmn = 7
        self.wdepth = WDEPTH
        self.Rsqb = Res("sqb")
        self.tlong = (self.tmp[:, 7, :], self.tmr[7])
        self.Rr = [[Res(f"r{k}_{t}") for t in range(NT)] for k in range(KC)]
        self.Rh = [[Res(f"h{k}_{t}") for t in range(NT)] for k in range(KC)]
        self.Rconst = Res("const")
        self.Rvec = Res("vecs")
        self.Rd = [Res(f"dg{i}") for i in range(NDIAG)]
        self.di = 0
        self.Rscr = [Res("scr")]
        self.Rmulti = [Res("multi")]
        self.Rstat = [Res(f"st{i}") for i in range(4)]
        self.sti = 0

    def ps(self):
        i = self.psi
        self.psi = (i + 1) % 8
        return self.psb[i], self.psr[i]

    def ps4(self):
        i = 0 if self.psi <= 0 or self.psi > 4 else 4
        self.psi = (i + 4) % 8
        return self.pst[:, i:i + 4, :], self.psr[i:i + 4]

    def tmp4(self):
        i = self.tm4
        self.tm4 = 4 - i
        return self.tmp[:, i:i + 4, :], self.tmr[i:i + 4]

    def tmpf(self):
        i = self.tmi
        self.tmi = (i + 1) % self.tmn
        return self.tmp[:, i, :], self.tmr[i]

    def retile(self, which, n):
        old = self.Rscr if which == "scr" else self.Rmulti
        carry = carry_of(old)
        new = []
        for i in range(n):
            r = Res(f"{which}{i}")
            r.rs = list(carry)
            new.append(r)
        if which == "scr":
            self.Rscr = new
        else:
            self.Rmulti = new
        return new

    def _issue_w(self, j, src, viewfn):
        slot = j % NSLOT
        view = viewfn(self.wring[:, slot, :])
        self.wcur[slot] = j
        if isinstance(src, (list, tuple)):
            for (sap, subfn) in src:
                sv = subfn(view)
                self.P.op("pool", lambda e, sv=sv, sap=sap: e.dma_start(out=sv, in_=sap),
                          writes=[self.Rslot[slot]], dma=True)
            return
        self.P.op("pool", lambda e, view=view, src=src: e.dma_start(out=view, in_=src),
                  writes=[self.Rslot[slot]], dma=True)

    def wget(self, src, viewfn):
        i = self.wi
        self.wi += 1
        self.plan_out.append((src, viewfn))
        if self.plan_in is None:
            self._issue_w(i, src, viewfn)
            self.wissued = i + 1
        else:
            lim = min(i + self.wdepth, len(self.plan_in))
            for j in range(self.wissued, lim):
                self._issue_w(j, *self.plan_in[j])
            self.wissued = max(self.wissued, lim)
        slot = i % NSLOT
        assert self.wcur[slot] == i
        return viewfn(self.wring[:, slot, :]), self.Rslot[slot], (slot, i)

    def wcheck(self, tag):
        assert self.wcur[tag[0]] == tag[1], "weight slot overwritten before use"

    def mm(self, ps, lhsT, rhs, start, stop, reads, pres):
        self.P.op("pe", lambda e: e.matmul(ps, lhsT, rhs, start=start, stop=stop),
                  reads=reads, writes=[pres])

    def act(self, out, in_, func, reads, writes, bias=None, scale=None):
        kw = {}
        if bias is not None:
            kw["bias"] = bias
        if scale is not None:
            kw["scale"] = scale
        self.P.op("act", lambda e: e.activation(out=out, in_=in_, func=func, **kw),
                  reads=reads, writes=writes)

    def stt(self, out, in0, scalar, in1, op0, op1, reads, writes, eng="dve"):
        self.P.op(eng, lambda e: e.scalar_tensor_tensor(out=out, in0=in0, scalar=scalar, in1=in1,
                                                         op0=op0, op1=op1),
                  reads=reads, writes=writes)

    def tt(self, out, in0, in1, op, reads, writes, eng="dve"):
        self.P.op(eng, lambda e: e.tensor_tensor(out=out, in0=in0, in1=in1, op=op),
                  reads=reads, writes=writes)

    def ts(self, out, in0, s1, s2, op0, op1, reads, writes, eng="dve"):
        if s2 is None:
            self.P.op(eng, lambda e: e.tensor_scalar(out=out, in0=in0, scalar1=s1, scalar2=None, op0=op0),
                      reads=reads, writes=writes)
        else:
            self.P.op(eng, lambda e: e.tensor_scalar(out=out, in0=in0, scalar1=s1, scalar2=s2,
                                                     op0=op0, op1=op1),
                      reads=reads, writes=writes)

    def vcol(self, name, j=0):
        c = VEC_LAYOUT[name] + j
        return self.vecs[:, c:c + 1]

    def diag_group(self, col, n, slot0):
        dst = self.dring[:, slot0:slot0 + n, :]
        in0 = self.identf[:].unsqueeze(1).broadcast_to([128, n, 128])
        in1 = self.vecs[:, col:col + n].unsqueeze(2).broadcast_to([128, n, 128])
        res = [self.Rd[slot0 + i] for i in range(n)]
        self.P.op("dve", lambda e: e.tensor_tensor(out=dst, in0=in0, in1=in1, op=ALU.mult),
                  reads=[self.Rconst, self.Rvec], writes=res)
        return [(self.dring[:, slot0 + i, :], res[i]) for i in range(n)]

    def diag6(self, col, n=6):
        g = self.di
        self.di = (g + 1) % 5
        return self.diag_group(col, n, 6 * g)

    def setup(self):
        P = self.P
        nc = self.nc
        vecs, vecs_d = self.vecs, self.vecs_d
        P.op("sp", lambda e: e.dma_start(out=vecs[:], in_=vecs_d), writes=[self.Rvec], dma=True)
        identf, ident, ones_m, epsT = self.identf, self.ident, self.ones_m, self.epsT
        P.op("pool", lambda e: e.memset(identf[:], 1.0), writes=[self.Rconst])
        P.op("pool", lambda e: e.affine_select(out=identf[:], in_=identf[:], pattern=[[-1, 128]],
                                               compare_op=ALU.is_equal, fill=0.0, base=0,
                                               channel_multiplier=1),
             reads=[self.Rconst], writes=[self.Rconst])
        P.op("pool", lambda e: e.tensor_copy(out=ident[:], in_=identf[:]), reads=[self.Rconst],
             writes=[self.Rconst])
        P.op("pool", lambda e: e.memset(ones_m[:], 1.0 / 1024.0), writes=[self.Rconst])
        P.op("pool", lambda e: e.memset(epsT[:], EPS), writes=[self.Rconst])
        t0, r0 = self.tmpf()
        wsT_d = self.wsT_d
        P.op("sp", lambda e: e.dma_start(out=t0, in_=wsT_d), writes=[r0], dma=True)
        t03 = t0.rearrange("p (a b) -> p a b", b=128)
        P.op("pool", lambda e: e.affine_select(out=t03, in_=t03, pattern=[[0, 4], [1, 128]],
                                               compare_op=ALU.is_ge, fill=0.0, base=0,
                                               channel_multiplier=-1),
             reads=[r0], writes=[r0])
        wmT = self.wmT
        self.Rwm = Res("wmT")
        P.op("dve", lambda e: e.tensor_copy(out=wmT[:], in_=t03), reads=[r0], writes=[self.Rwm])
        t1, r1 = self.tmpf()
        bsbc_d = self.bsbc_d
        P.op("sp", lambda e: e.dma_start(out=t1, in_=bsbc_d), writes=[r1], dma=True)
        ps, pr = self.ps()
        wm2 = wmT[:].rearrange("p a b -> p (a b)")
        self.mm(ps[:], ones_m[:], wm2, True, True, [self.Rconst, self.Rwm], pr)
        t2, r2 = self.tmpf()
        self.RQ = Res("Q")
        for hd in range(4):
            sl = slice(hd * 128, (hd + 1) * 128)
            self.ts(t2[:, sl], ps[:, sl], self.vcol("ev_a_ln_b", hd), None, ALU.mult, None,
                    [pr, self.Rvec], [r2])
        Qf = self.Q[:].rearrange("p a b -> p (a b)")
        self.stt(Qf, t2, 1024.0, t1, ALU.mult, ALU.add, [r2, r1], [self.RQ])

    def load_x_tt(self, s, tt):
        r = self.r
        src = self.xT.rearrange("(k p) n -> p k n", p=128)
        a, b = s * S + tt * TT, s * S + (tt + 1) * TT
        dst = r[:, :, tt * TT:(tt + 1) * TT]
        self.P.op("sp", lambda e, dst=dst, sv=src[:, :, a:b]: e.dma_start(out=dst, in_=sv),
                  writes=[self.Rr[k][tt] for k in range(KC)], dma=True)

    def load_x(self, s):
        for tt in range(NT):
            self.load_x_tt(s, tt)

    def norm(self, gname):
        for tt in range(NT):
            self._norm_one(tt, gname)

    def out_proj_tt(self, wsrc, src_ap_fn, src_res_fn, bias_name=None, hook=None):
        r = self.r
        self.wdepth = 1
        ws_ = []
        for cp in range(4):
            if cp == 3:
                self.wdepth = 2
            ws_.append(self.wget(wsrc.rearrange("(k p) n -> p k n", p=128)[:, :, cp * 256:(cp + 1) * 256],
                                 lambda s_: s_.rearrange("p (k n) -> p k n", k=KC)))
        self.wdepth = WDEPTH
        for tt in range(NT):
            sl = slice(tt * TT, (tt + 1) * TT)
            for cp in range(4):
                wv, wr, tag = ws_[cp]
                for b in range(2):
                    c = 2 * cp + b
                    ps, pr = self.ps()
                    self.wcheck(tag)
                    for k in range(KC):
                        self.mm(ps[:], wv[:, k, b * 128:(b + 1) * 128], src_ap_fn(k, tt), k == 0, k == KC - 1,
                                [wr, src_res_fn(k, tt)], pr)
                    rr = self.Rr[c][tt]
                    if bias_name is None:
                        self.tt(r[:, c, sl], ps[:], r[:, c, sl], ALU.add, [pr, rr], [rr])
                    else:
                        self.stt(r[:, c, sl], ps[:], self.vcol(bias_name, c), r[:, c, sl], ALU.add, ALU.add,
                                 [pr, rr, self.Rvec], [rr])
            if hook is not None and tt >= 1:
                hook(tt - 1)
        if hook is not None:
            hook(NT - 1)

    def out_proj(self, wsrc, src_ap_fn, src_res_fn, bias_name=None, hook=None):
        r = self.r
        for cp in range(4):
            wv, wr, tag = self.wget(wsrc.rearrange("(k p) n -> p k n", p=128)[:, :, cp * 256:(cp + 1) * 256],
                                    lambda s: s.rearrange("p (k n) -> p k n", k=KC))
            for tt in range(NT):
                sl = slice(tt * TT, (tt + 1) * TT)
                for b in range(2):
                    c = 2 * cp + b
                    ps, pr = self.ps()
                    self.wcheck(tag)
                    for k in range(KC):
                        self.mm(ps[:], wv[:, k, b * 128:(b + 1) * 128], src_ap_fn(k, tt), k == 0, k == KC - 1,
                                [wr, src_res_fn(k, tt)], pr)
                    rr = self.Rr[c][tt]
                    if bias_name is None:
                        self.tt(r[:, c, sl], ps[:], r[:, c, sl], ALU.add, [pr, rr], [rr])
                    else:
                        self.stt(r[:, c, sl], ps[:], self.vcol(bias_name, c), r[:, c, sl], ALU.add, ALU.add,
                                 [pr, rr, self.Rvec], [rr])
                if cp == 3 and hook is not None and tt >= 1:
                    hook(tt - 1)
            if cp == 3 and hook is not None:
                hook(NT - 1)

    def ffn(self, i, s, hook=None):
        P = self.P
        h, r = self.h, self.r
        Rm = self.retile("multi", 7)
        zsb = self.multi[:, 0:2050]
        Rz = Rm[0:4]
        Rzh = Rm[4]
        P.op("dve", lambda e: e.memset(zsb[:, 0:2], 0.0), writes=[Rzh])
        hal = self.multi[:, 2056:2064].bitcast(F32)
        halo = [hal[:, 0:2], hal[:, 2:4]]
        Rhalo = Rm[5:7]
        bcol = VEC_LAYOUT[f"ffn_dw_b{i}"]
        A = self.scr[:, :].rearrange("p (j n) -> p j n", n=S)
        wup = self.w_up[i].rearrange("(k p) n -> p k n", p=128)
        wdn = self.w_down[i].rearrange("(j p) n -> p j n", p=128)
        dwc = VEC_LAYOUT[f"ffn_dw_w{i}"]
        for (j0, nb) in ((0, 12), (12, 10)):
            Rs = self.retile("scr", nb * NT)
            Ra = [[Rs[j * NT + t] for t in range(NT)] for j in range(nb)]
            tiles = []
            for pr_ in range(nb // 2):
                for b in range(2):
                    for tt in range(NT):
                        tiles.append((pr_, b, tt))
            state = {}

            def U(tile):
                pr_, b, tt = tile
                jl = 2 * pr_ + b
                jj = j0 + jl
                if b == 0 and tt == 0:
                    g0 = jj * 128
                    state["G"] = self.wget(wup[:, :, g0:g0 + 256], lambda s_: s_.rearrange("p (k n) -> p k n", k=KC))
                    state["U"] = self.wget(wup[:, :, DFF + g0:DFF + g0 + 256],
                                           lambda s_: s_.rearrange("p (k n) -> p k n", k=KC))
                if tt == 0:
                    state[("du", jl)] = self.diag6(dwc + jj * 6 + 3, 3)
                sl = slice(tt * TT, (tt + 1) * TT)
                zs = slice(2 + tt * TT, 2 + (tt + 1) * TT)
                wv, wr, tag = state["G"]
                self.wcheck(tag)
                ps, pr = self.ps()
                for k in range(KC):
                    self.mm(ps[:], wv[:, k, b * 128:(b + 1) * 128], h[:, k, sl], k == 0, k == KC - 1,
                            [wr, self.Rh[k][tt]], pr)
                w0, w1, w2 = [self.vecs[:, dwc + jj * 6 + t_:dwc + jj * 6 + t_ + 1] for t_ in range(3)]
                acc, racc = self.tmpf()
                self.act(acc, ps[:], AF.Identity, [pr, self.Rvec], [racc],
                         bias=self.vecs[:, bcol + jj:bcol + jj + 1], scale=w2)
                if tt < NT - 1:
                    P.op("dve", lambda e, o=halo[tt % 2], p_=ps: e.tensor_copy(out=o, in_=p_[:, TT - 2:TT]),
                         reads=[pr, racc], writes=[Rhalo[tt % 2]])
                self.stt(acc[:, 1:TT], ps[:, 0:TT - 1], w1, acc[:, 1:TT], ALU.mult, ALU.add,
                         [pr, racc, self.Rvec], [racc])
                self.stt(acc[:, 2:TT], ps[:, 0:TT - 2], w0, acc[:, 2:TT], ALU.mult, ALU.add,
                         [pr, racc, self.Rvec], [racc])
                if tt > 0:
                    hp, rhp = halo[(tt - 1) % 2], Rhalo[(tt - 1) % 2]
                    self.stt(acc[:, 0:1], hp[:, 1:2], w1, acc[:, 0:1], ALU.mult, ALU.add,
                             [rhp, racc, self.Rvec], [racc])
                    self.stt(acc[:, 0:2], hp[:, 0:2], w0, acc[:, 0:2], ALU.mult, ALU.add,
                             [rhp, racc, self.Rvec], [racc])
                state[("sg", jl, tt)] = (acc, racc)
                wv, wr, tag = state["U"]
                self.wcheck(tag)
                ps, pr = self.ps()
                for k in range(KC):
                    self.mm(ps[:], wv[:, k, b * 128:(b + 1) * 128], h[:, k, sl], k == 0, k == KC - 1,
                            [wr, self.Rh[k][tt]], pr)
                self.act(zsb[:, zs], ps[:], AF.Copy, [pr], [Rz[tt]])
                flush_silu()
                state["pend_silu"] = (acc, racc)

            def flush_silu():
                pnd = state.pop("pend_silu", None)
                if pnd is not None:
                    self.act(pnd[0], pnd[0], AF.Silu, [pnd[1]], [pnd[1]])

            def C(tile):
                pr_, b, tt = tile
                jl = 2 * pr_ + b
                jj = j0 + jl
                sl = slice(tt * TT, (tt + 1) * TT)
                ps, pr = self.ps()
                dgs = state[("du", jl)]
                rd = [Rz[tt]] + ([Rzh] if tt == 0 else [Rz[tt - 1]])
                for tap in range(3):
                    self.mm(ps[:], dgs[tap][0], zsb[:, tt * TT + tap: tt * TT + tap + TT],
                            tap == 0, tap == 2, rd + [dgs[tap][1]], pr)
                sg, rsg = state.pop(("sg", jl, tt))
                self.stt(A[:, jl, sl], ps[:], self.vecs[:, bcol + 22 + jj:bcol + 22 + jj + 1], sg,
                         ALU.add, ALU.mult, [pr, rsg, self.Rvec], [Ra[jl][tt]])

            pend = None
            for tile in tiles:
                U(tile)
                if pend is not None:
                    C(pend)
                pend = tile
            flush_silu()
            C(pend)
            if j0 != 0:
                self.load_p(i, s)
            dview = lambda s_, nb=nb: s_[:, 0:nb * 128].rearrange("p (j n) -> p j n", j=nb)
            for cp in range(4):
                ws_ = [self.wget(wdn[:, j0:j0 + nb, c * 128:(c + 1) * 128], dview) for c in (2 * cp, 2 * cp + 1)]
                for tt in range(NT):
                    sl = slice(tt * TT, (tt + 1) * TT)
                    for b in range(2):
                        c = 2 * cp + b
                        wv, wr, tag = ws_[b]
                        ps, pr = self.ps()
                        self.wcheck(tag)
                        for jl in range(nb):
                            self.mm(ps[:], wv[:, jl, :], A[:, jl, sl], jl == 0, jl == nb - 1,
                                    [wr, Ra[jl][tt]], pr)
                        rr = self.Rr[c][tt]
                        self.tt(r[:, c, sl], ps[:], r[:, c, sl], ALU.add, [pr, rr], [rr])
                    if j0 != 0 and cp == 3 and hook is not None and tt >= 1:
                        hook(tt - 1)
                if j0 != 0 and cp == 3 and hook is not None:
                    hook(NT - 1)

    def load_p(self, i, s):
        Rm = self.retile("multi", 1)
        self.Rp = Rm[0]
        pb = self.multi[:, 0:4096].rearrange("p (k n) -> p k n", k=2)
        src = self.pT[i].rearrange("(k p) n -> p k n", p=128)[:, :, s * S:(s + 1) * S]
        self.P.op("pool", lambda e: e.dma_start(out=pb, in_=src), writes=[self.Rp], dma=True)

    def ple(self, i, s, hook=None):
        r, h = self.r, self.h
        pb = self.multi[:, 0:4096].rearrange("p (k n) -> p k n", k=2)
        wpp = self.w_pp[i].rearrange("(k p) n -> p k n", p=128)
        wg = self.w_pg[i].rearrange("(k p) n -> p k n", p=128)
        for cp in range(4):
            wv, wr, tag = self.wget(wg[:, :, cp * 256:(cp + 1) * 256],
                                    lambda s_: s_.rearrange("p (k n) -> p k n", k=KC))
            wp, wpr, wptag = self.wget(wpp[:, :, cp * 256:(cp + 1) * 256],
                                       lambda s_: s_[:, 0:512].rearrange("p (k n) -> p k n", k=2))
            for tt in range(NT):
                sl = slice(tt * TT, (tt + 1) * TT)
                if cp == 3 and hook is not None and tt >= 1:
                    hook(tt - 1, 0)
                for b in range(2):
                    c = 2 * cp + b
                    self.wcheck(tag)
                    self.wcheck(wptag)
                    psg, prg = self.ps()
                    for k in range(KC):
                        self.mm(psg[:], wv[:, k, b * 128:(b + 1) * 128], h[:, k, sl], k == 0, k == KC - 1,
                                [wr, self.Rh[k][tt]], prg)
                    psp, prp = self.ps()
                    for k in range(2):
                        self.mm(psp[:], wp[:, k, b * 128:(b + 1) * 128], pb[:, k, sl], k == 0, k == 1,
                                [wpr, self.Rp], prp)
                    t, tr = self.tmpf()
                    self.act(t, psg[:], AF.Sigmoid, [prg], [tr])
                    self.tt(t, psp[:], t, ALU.mult, [prp, tr], [tr])
                    rr = self.Rr[c][tt]
                    self.tt(r[:, c, sl], t, r[:, c, sl], ALU.add, [tr, rr], [rr])
                if cp == 3 and hook is not None and tt >= 1:
                    hook(tt - 1, 1)
            if cp == 3 and hook is not None:
                hook(NT - 1)

    def mix0(self, s, hook=None, prenorm=False):
        P = self.P
        r, h = self.r, self.h
        Rs = self.retile("scr", 8 * NT + 16)
        Rmo = [[Rs[j * NT + t] for t in range(NT)] for j in range(8)]
        Rvn = [Rs[32 + ch] for ch in range(16)]
        MO = self.scr[:, 0:16384].rearrange("p (j n) -> p j n", n=S)
        VN = self.scr[:, 16384:24576].rearrange("p (c n) -> p c n", n=512)
        win = self.w_ev_in.rearrange("(k p) n -> p k n", p=128)
        kview = lambda s_: s_.rearrange("p (k n) -> p k n", k=KC)
        stat = self.stat[:].rearrange("p a b -> p (a b)")
        rst = self.Rstat[0]
        Rm = self.retile("multi", 6)
        s4 = self.multi[:, 2080:2080 + 2048].rearrange("p (c n) -> p c n", c=4)
        rs4 = [Rm[5]]
        v4 = self.tmp[:, 4:8, :]
        rv4 = self.tmr[4:8]
        self.tmn = 4
        self.tmi = 0
        xsb = self.multi[:, 0:2050]
        Rx, Rxh = Rm[0:4], Rm[4]
        P.op("dve", lambda e: e.memset(xsb[:, 0:2], 0.0), writes=[Rxh])
        cwc = VEC_LAYOUT["ev_b_conv_w"]

        def Bblock(kb, after):
            ch_ = self.wget([(win[:, :, 1536 + kb * 128:1536 + (kb + 1) * 128], lambda v: v[:, :, 0, :]),
                             (win[:, :, 2048 + kb * 128:2048 + (kb + 1) * 128], lambda v: v[:, :, 1, :])],
                            lambda s_: s_.rearrange("p (k g n) -> p k g n", k=KC, g=2))
            bbw = self.wget(win[:, :, 1024 + kb * 128:1024 + (kb + 1) * 128],
                            lambda s_: s_[:, 0:1024].rearrange("p (k n) -> p k n", k=KC))
            dgs = self.diag6(cwc + kb * 3, 3)
            for tt in range(NT):
                sl = slice(tt * TT, (tt + 1) * TT)
                self.wcheck(ch_[2])
                self.wcheck(bbw[2])
                pH, rH = self.ps()
                for k in range(KC):
                    self.mm(pH[:], ch_[0][:, k, 1, :], h[:, k, sl], k == 0, k == KC - 1, [ch_[1], self.Rh[k][tt]], rH)
                pC, rC = self.ps()
                for k in range(KC):
                    self.mm(pC[:], ch_[0][:, k, 0, :], h[:, k, sl], k == 0, k == KC - 1, [ch_[1], self.Rh[k][tt]], rC)
                pB, rB = self.ps()
                for k in range(KC):
                    self.mm(pB[:], bbw[0][:, k, :], h[:, k, sl], k == 0, k == KC - 1, [bbw[1], self.Rh[k][tt]], rB)
                hh, rhh = self.tmpf()
                self.act(hh, pH[:], AF.Copy, [rH], [rhh])
                self.tt(xsb[:, 2 + tt * TT:2 + (tt + 1) * TT], pC[:], hh, ALU.mult, [rC, rhh], [Rx[tt]])
                pV, rV = self.ps()
                rd = [Rx[tt]] + ([Rxh] if tt == 0 else [Rx[tt - 1]])
                for tap in range(3):
                    self.mm(pV[:], dgs[tap][0], xsb[:, tt * TT + tap:tt * TT + tap + TT], tap == 0, tap == 2,
                            rd + [dgs[tap][1]], rV)
                bb, rbb = self.tmpf()
                self.act(bb, pB[:], AF.Copy, [rB], [rbb])
                self.tt(MO[:, 4 + kb, sl], pV[:], bb, ALU.mult, [rV, rbb], [Rmo[4 + kb][tt]])
                after[tt]()

        def Vphases(tt):
            v16 = v4.rearrange("p c (a b) -> p (c a) b", b=128)
            s16 = s4.rearrange("p c (a b) -> p (c a) b", b=128)

            def ph0():
                va = self.wget(win[:, :, 512:768], kview)
                vb = self.wget(win[:, :, 768:1024], kview)
                P4, r4 = self.ps4()
                for q in range(4):
                    ch = tt * 4 + q
                    tok = slice(ch * 128, (ch + 1) * 128)
                    for half, wv in ((0, va), (1, vb)):
                        self.wcheck(wv[2])
                        for k in range(KC):
                            self.mm(P4[:, q, half * 256:(half + 1) * 256], h[:, k, tok], wv[0][:, k, :], k == 0,
                                    k == KC - 1, [wv[1], self.Rh[k][tt]], r4[q])
                self.act(v4, P4, AF.Gelu_apprx_tanh, list(r4), list(rv4))
                self.act(s4, v4, AF.Square, list(rv4), list(rs4))

            def ph1():
                P.op("dve", lambda e: e.tensor_reduce(out=stat[:, 0:16], in_=v16, axis=AX.X, op=ALU.add),
                     reads=list(rv4), writes=[rst])
                P.op("dve", lambda e: e.tensor_reduce(out=stat[:, 16:32], in_=s16, axis=AX.X, op=ALU.add),
                     reads=list(rs4), writes=[rst])

            def ph2():
                self.ts(stat[:, 32:48], stat[:, 0:16], 1.0 / 128.0, None, ALU.mult, None, [rst], [rst])
                self.tt(stat[:, 48:64], stat[:, 32:48], stat[:, 32:48], ALU.mult, [rst], [rst])
                self.stt(stat[:, 16:32], stat[:, 16:32], 1.0 / 128.0, stat[:, 48:64], ALU.mult, ALU.subtract,
                         [rst], [rst])
                self.ts(stat[:, 16:32], stat[:, 16:32], 0.0, None, ALU.max, None, [rst], [rst])
                self.act(stat[:, 16:32], stat[:, 16:32], AF.Ln, [rst, self.Rconst], [rst], bias=self.epsT[:, 0:1],
                         scale=1.0)
                self.act(stat[:, 16:32], stat[:, 16:32], AF.Exp, [rst], [rst], scale=-0.5)

            def ph3():
                mean_b = stat[:, 32:48].unsqueeze(2).broadcast_to([128, 16, 128])
                rstd_b = stat[:, 16:32].unsqueeze(2).broadcast_to([128, 16, 128])
                self.tt(v16, v16, mean_b, ALU.subtract, list(rv4) + [rst], list(rv4))
                vn16 = VN[:, tt * 4:(tt + 1) * 4, :].rearrange("p c (a b) -> p (c a) b", b=128)
                self.tt(vn16, v16, rstd_b, ALU.mult, list(rv4) + [rst], [Rvn[tt * 4 + q] for q in range(4)])

            return [ph0, ph1, ph2, ph3]

        self.wdepth = 2
        for kb in range(4):
            ph = Vphases(kb)
            if kb == 0 and prenorm:
                self._norm_one(0, "ev_norm")
                self._norm_one(1, "ev_norm")

                def mk(tt, f):
                    def g():
                        if tt + 2 < NT:
                            self._norm_one(tt + 2, "ev_norm")
                        f()
                    return g
                ph = [mk(tt, ph[tt]) for tt in range(NT)]
            Bblock(kb, ph)
        self.wdepth = WDEPTH
        self.tmn = 7
        Q, wmT = self.Q, self.wmT
        for hp in range(2):
            ua = self.wget(win[:, :, hp * 256:(hp + 1) * 256], kview)
            for b in range(2):
                hd = 2 * hp + b
                for tt in range(NT):
                    sl = slice(tt * TT, (tt + 1) * TT)
                    self.wcheck(ua[2])
                    ps, pr = self.ps()
                    for k in range(KC):
                        self.mm(ps[:], ua[0][:, k, b * 128:(b + 1) * 128], h[:, k, sl], k == 0, k == KC - 1,
                                [ua[1], self.Rh[k][tt]], pr)
                    u, ru = self.tmpf()
                    self.act(u, ps[:], AF.Gelu_apprx_tanh, [pr], [ru])
                    pm, prm = self.ps()
                    for q in range(4):
                        ch = tt * 4 + q
                        self.mm(pm[:, q * 128:(q + 1) * 128], VN[:, ch, hd * 128:(hd + 1) * 128], wmT[:, hd, :],
                                True, True, [Rvn[ch], self.Rwm], prm)
                    t, tr = self.tmpf()
                    qb = Q[:, hd, :].unsqueeze(1).broadcast_to([128, 4, 128])
                    self.stt(t.rearrange("p (a b) -> p a b", b=128), pm[:].rearrange("p (a b) -> p a b", b=128),
                             self.vcol("ev_a_ln_g", hd), qb, ALU.mult, ALU.add, [prm, self.RQ, self.Rvec], [tr])
                    self.tt(MO[:, hd, sl], t, u, ALU.mult, [tr, ru], [Rmo[hd][tt]])
        self.out_proj(self.w_ev_out, lambda k, tt: MO[:, k, tt * TT:(tt + 1) * TT], lambda k, tt: Rmo[k][tt],
                      hook=hook)

    def mix1(self, s, hook=None):
        P = self.P
        r, h = self.r, self.h
        Rs = self.retile("scr", 8 * NT)
        Ry = [[Rs[j * NT + t] for t in range(NT)] for j in range(8)]
        Y = self.scr[:, 0:16384].rearrange("p (j n) -> p j n", n=S)
        Rm = self.retile("multi", 10)
        XW = 2080
        xsb = [self.multi[:, 0:XW], self.multi[:, XW:2 * XW]]
        Rx = [[Rm[bb * 5 + t] for t in range(NT)] for bb in range(2)]
        Rxh = [Rm[4], Rm[9]]
        for bb in range(2):
            P.op("dve", lambda e, bb=bb: e.memset(xsb[bb][:, 0:30], 0.0), writes=[Rxh[bb]])
        win = self.w_od_in.rearrange("(k p) n -> p k n", p=128)
        dwc = VEC_LAYOUT["od_dw_w"]
        st = {}
        NPE = 24

        def IN(k):
            bb = k % 2
            ag = self.wget([(win[:, :, k * 128:(k + 1) * 128], lambda v: v[:, :, 0, :]),
                            (win[:, :, 1024 + k * 128:1024 + (k + 1) * 128], lambda v: v[:, :, 1, :])],
                           lambda s_: s_.rearrange("p (k g n) -> p k g n", k=KC, g=2))
            for tt in range(NT):
                sl = slice(tt * TT, (tt + 1) * TT)
                if tt == 2:
                    st[k] = self.diag_group(dwc + k * 31, 31 if k == KC - 1 else NPE, 0)
                self.wcheck(ag[2])
                pA, rA = self.ps()
                for kc in range(KC):
                    self.mm(pA[:], ag[0][:, kc, 0, :], h[:, kc, sl], kc == 0, kc == KC - 1, [ag[1], self.Rh[kc][tt]], rA)
                pG, rG = self.ps()
                for kc in range(KC):
                    self.mm(pG[:], ag[0][:, kc, 1, :], h[:, kc, sl], kc == 0, kc == KC - 1, [ag[1], self.Rh[kc][tt]], rG)
                sg, rsg = self.tmpf()
                self.act(sg, pG[:], AF.Sigmoid, [rG, self.Rvec], [rsg], bias=self.vcol("od_b_in", 8 + k), scale=1.0)
                self.stt(xsb[bb][:, 30 + tt * TT:30 + (tt + 1) * TT], pA[:], self.vcol("od_b_in", k), sg,
                         ALU.add, ALU.mult, [rA, rsg, self.Rvec], [Rx[bb][tt]])

        def CONV(k, ln=None):
            bb = k % 2
            dgs = st[k]
            for tt in range(NT):
                if ln is not None and tt >= 2:
                    ln(tt - 2)
                sl = slice(tt * TT, (tt + 1) * TT)
                pY, rY = self.ps()
                rd = [Rx[bb][tt]] + ([Rxh[bb]] if tt == 0 else [Rx[bb][tt - 1]])
                npe = 31 if k == KC - 1 else NPE
                for tap in range(npe):
                    self.mm(pY[:], dgs[tap][0], xsb[bb][:, tt * TT + tap:tt * TT + tap + TT], tap == 0, tap == npe - 1,
                            rd + [dgs[tap][1]], rY)
                if npe == 31:
                    self.act(Y[:, k, sl], pY[:], AF.Identity, [rY, self.Rvec], [Ry[k][tt]],
                             bias=self.vcol("od_dw_b", k), scale=1.0)
                    continue
                acc, racc = self.tmpf()
                wcol = lambda j: self.vecs[:, dwc + k * 31 + j:dwc + k * 31 + j + 1]
                self.act(acc, xsb[bb][:, tt * TT + 30:tt * TT + 30 + TT], AF.Identity, rd + [self.Rvec], [racc],
                         scale=wcol(30))
                for j in range(npe, 30):
                    self.stt(acc, xsb[bb][:, tt * TT + j:tt * TT + j + TT], wcol(j), acc, ALU.mult, ALU.add,
                             rd + [racc, self.Rvec], [racc])
                self.stt(Y[:, k, sl], pY[:], self.vcol("od_dw_b", k), acc, ALU.add, ALU.add,
                         [rY, racc, self.Rvec], [Ry[k][tt]])
            if ln is not None:
                ln(NT - 2)
                ln(NT - 1)

        lnst = {}
        longs = [self.tlong, (self.tmp[:, 6, :], self.tmr[6])]

        def LNa(tt):
            sl = slice(tt * TT, (tt + 1) * TT)
            yres = [Ry[k][tt] for k in range(KC)]
            hres = [self.Rh[k][tt] for k in range(KC)]
            p1, r1 = self.ps()
            for k in range(KC):
                self.mm(p1[:], self.ones_m[:], Y[:, k, sl], k == 0, k == KC - 1, [self.Rconst, yres[k]], r1)
            self.act(h[:, :, sl], Y[:, :, sl], AF.Square, yres, hres)
            p2, r2 = self.ps()
            for k in range(KC):
                self.mm(p2[:], self.ones_m[:], h[:, k, sl], k == 0, k == KC - 1, [self.Rconst, hres[k]], r2)
            mq, rmq = longs[tt % 2]
            self.act(mq, p1[:], AF.Square, [r1], [rmq])
            self.tt(mq, p2[:], mq, ALU.subtract, [r2, rmq], [rmq])
            self.ts(mq, mq, 0.0, None, ALU.max, None, [rmq], [rmq])
            self.act(mq, mq, AF.Ln, [rmq, self.Rconst], [rmq], bias=self.epsT[:, 0:1], scale=1.0)
            self.act(mq, mq, AF.Exp, [rmq], [rmq], scale=-0.5)
            lnst[tt] = (p1, r1, mq, rmq)

        def LNb(tt):
            sl = slice(tt * TT, (tt + 1) * TT)
            yres = [Ry[k][tt] for k in range(KC)]
            p1, r1, mq, rmq = lnst.pop(tt)
            for k in range(KC):
                t, tr = self.tmpf()
                self.tt(t, Y[:, k, sl], p1[:], ALU.subtract, [yres[k], r1], [tr])
                self.tt(t, t, mq, ALU.mult, [tr, rmq], [tr])
                self.act(Y[:, k, sl], t, AF.Silu, [tr, self.Rvec], [yres[k]],
                         bias=self.vcol("od_ln_b", k), scale=self.vcol("od_ln_g", k))

        def LN(tt):
            LNa(tt)
            if tt >= 1:
                LNb(tt - 1)
            if tt == NT - 1:
                LNb(tt)

        self.tmn = 6
        self.tmi = 0
        for k in range(KC):
            IN(k)
            CONV(k, LN if k == KC - 1 else None)
        self.out_proj_tt(self.w_od_out, lambda k, tt: Y[:, k, tt * TT:(tt + 1) * TT], lambda k, tt: Ry[k][tt],
                         bias_name="od_b_out", hook=hook)
        self.tmn = 7

    def final_prep(self):
        Rs = self.retile("scr", 2)
        O = [self.scr[:, 0:8192].bitcast(F32).rearrange("p (k n) -> p k n", k=KC),
             self.scr[:, 8192:16384].bitcast(F32).rearrange("p (k n) -> p k n", k=KC)]
        self.fin = [(O[tt % 2], Rs[tt % 2]) for tt in range(NT)]

    def final_tt(self, s, tt, phase=None):
        dst = self.oT.rearrange("(k p) n -> p k n", p=128)
        self._norm_one(tt, "final_norm", self.fin[tt], phase=phase)
        if phase == 0:
            return
        a = s * S + tt * TT
        o = self.P.op("sp", lambda e, src=self.fin[tt][0], d=dst[:, :, a:a + TT]: e.dma_start(out=d, in_=src),
                      reads=[self.fin[tt][1]], dma=True)
        self.P.out_dmas.append(o)
        if s + 1 < self.nseq:
            self.load_x_tt(s + 1, tt)

    def final_raw(self, s):
        dst = self.oT.rearrange("(k p) n -> p k n", p=128)
        for tt in range(NT):
            a = s * S + tt * TT
            src = self.r[:, :, tt * TT:(tt + 1) * TT]
            o = self.P.op("sp", lambda e, src=src, d=dst[:, :, a:a + TT]: e.dma_start(out=d, in_=src),
                          reads=[self.Rr[k][tt] for k in range(KC)], dma=True)
            self.P.out_dmas.append(o)

    def _norm_one(self, tt, gname, out=None, phase=None):
        r, h = self.r, self.h
        sl = slice(tt * TT, (tt + 1) * TT)
        rres = [self.Rr[k][tt] for k in range(KC)]
        hres = [self.Rh[k][tt] for k in range(KC)]
        if phase in (None, 0):
            self.act(h[:, :, sl], r[:, :, sl], AF.Square, rres, hres)
        if phase == 0:
            return
        ps, pr = self.ps()
        for k in range(KC):
            self.mm(ps[:], self.ones_m[:], h[:, k, sl], k == 0, k == KC - 1, [self.Rconst, hres[k]], pr)
        t, tr = self.tmpf()
        self.act(t, ps[:], AF.Ln, [pr, self.Rconst], [tr], bias=self.epsT[:, 0:1], scale=1.0)
        self.act(t, t, AF.Exp, [tr], [tr], scale=-0.5)
        for k in range(KC):
            if out is None:
                self.stt(h[:, k, sl], r[:, k, sl], self.vcol(gname, k), t, ALU.mult, ALU.mult,
                         [rres[k], tr, self.Rvec], [hres[k]])
            else:
                oap, ores = out
                self.stt(oap[:, k, :], r[:, k, sl], self.vcol(gname, k), t, ALU.mult, ALU.mult,
                         [rres[k], tr, self.Rvec, hres[k]], [ores])

    def record(self, plan):
        self.reset(plan)
        self.setup()
        names = ["ev_norm", "ffn_norm0", "ple_norm0", "od_norm", "ffn_norm1", "ple_norm1"]
        stages = [lambda s, hk: self.mix0(s, hk, prenorm=True), lambda s, hk: self.ffn(0, s, hk), lambda s, hk: self.ple(0, s, hk),
                  lambda s, hk: self.mix1(s, hk), lambda s, hk: self.ffn(1, s, hk), lambda s, hk: self.ple(1, s, hk)]
        n = min(self.nstages, 6)
        for s in range(self.nseq):
            if s == 0 or self.nstages < 7:
                self.load_x(s)
            for i in range(n):
                if i + 1 < n:
                    hk = lambda tt, phase=None, g=names[i + 1]: self._norm_one(tt, g, phase=phase)
                elif self.nstages >= 7:
                    self.final_prep()
                    hk = lambda tt, phase=None, s=s: self.final_tt(s, tt, phase)
                else:
                    hk = None
                stages[i](s, hk)
            if self.nstages < 7:
                self.final_raw(s)
        return self.plan_out

    def build(self):
        plan = self.record(None)
        self.record(plan)
        self.P.emit(self.nc)
        return self.nc


def make_in_maps(inp):
    x = np.asarray(inp["x"], np.float32)
    p = np.asarray(inp["p"], np.float32)
    vecs = pack_vecs(inp)
    ws = np.asarray(inp["ev_a_ws"], np.float32)[0]
    wsT = np.ascontiguousarray(ws.transpose(2, 0, 1)).reshape(128, 512)
    bs = np.asarray(inp["ev_a_bs"], np.float32)[0].reshape(1, 512)
    bsbc = np.ascontiguousarray(np.broadcast_to(bs, (128, 512)))
    shared = {
        "w_ev_in": np.ascontiguousarray(np.asarray(inp["ev_w_in"], np.float32)[0]),
        "w_ev_out": np.ascontiguousarray(np.asarray(inp["ev_w_out"], np.float32)[0]),
        "w_od_in": np.ascontiguousarray(np.asarray(inp["od_w_in"], np.float32)[0]),
        "w_od_out": np.ascontiguousarray(np.asarray(inp["od_w_out"], np.float32)[0]),
        "w_up": np.ascontiguousarray(np.asarray(inp["ffn_w_up"], np.float32)),
        "w_down": np.ascontiguousarray(np.asarray(inp["ffn_w_down"], np.float32)),
        "w_pp": np.ascontiguousarray(np.asarray(inp["ple_w_p"], np.float32)),
        "w_pg": np.ascontiguousarray(np.asarray(inp["ple_w_g"], np.float32)),
        "vecs": vecs, "wsT": wsT, "bsbc": bsbc,
    }
    maps = []
    for c in range(NCORE):
        xs = x[NSEQ * c:NSEQ * (c + 1)]
        xT = np.ascontiguousarray(xs.reshape(NSEQ * S, D).T)
        ps_ = p[:, NSEQ * c:NSEQ * (c + 1)]
        pT = np.ascontiguousarray(ps_.reshape(2, NSEQ * S, 256).transpose(0, 2, 1))
        m = dict(shared)
        m["xT"] = xT
        m["pT"] = pT
        maps.append(m)
    return maps


_NC_CACHE = {}


def get_nc(nstages=7, nseq=NSEQ):
    key = (nstages, nseq)
    if key not in _NC_CACHE:
        _NC_CACHE[key] = Builder(nstages, nseq).build()
    return _NC_CACHE[key]


def kernel(**inputs):
    nc = get_nc()
    maps = make_in_maps(inputs)
    res = run_bass_kernel_spmd(nc, maps, core_ids=list(range(NCORE)))
    out = np.empty((NCORE * NSEQ, S, D), np.float32)
    for c in range(NCORE):
        oT = np.asarray(res.results[c]["oT"])
        out[NSEQ * c:NSEQ * (c + 1)] = oT.T.reshape(NSEQ, S, D)
    return out
```

```python
import numpy as np
import concourse.bass as bass
import concourse.mybir as mybir
from concourse.bass_utils import run_bass_kernel_spmd

F32 = mybir.dt.float32
BF16 = mybir.dt.bfloat16
AF = mybir.ActivationFunctionType
ALU = mybir.AluOpType
AX = mybir.AxisListType

NCORE = 8
S = 2048
NT = 4
TT = 512
KC = 8
D = 1024
DFF = 2816
NSEQ = 2
EPS = 1e-6
NSLOT = 5
WDEPTH = 3
NDIAG = 32

ENGS = ("pe", "act", "dve", "pool", "sp")


class Res:
    __slots__ = ("name", "w", "rs", "psum")

    def __init__(self, name="", psum=False):
        self.name = name
        self.w = None
        self.rs = []
        self.psum = psum


class Op:
    __slots__ = ("eng", "idx", "fn", "deps", "need_sig", "sig", "dma", "dsem", "dval")

    def __init__(self, eng, idx, fn, dma):
        self.eng = eng
        self.idx = idx
        self.fn = fn
        self.deps = []
        self.need_sig = False
        self.sig = 0
        self.dma = dma
        self.dsem = -1
        self.dval = 0


class Prog:
    def __init__(self, n_dma_sems=12):
        self.ops = {e: [] for e in ENGS}
        self.n_dma_sems = n_dma_sems
        self.dma_count = {e: 0 for e in ENGS}
        self.dma_last = {}
        self.out_dmas = []

    def op(self, eng, fn, reads=(), writes=(), dma=False):
        o = Op(eng, len(self.ops[eng]), fn, dma)
        deps = {}
        for r in reads:
            if r.w is not None:
                deps[id(r.w)] = r.w
            if r.psum:
                for rd in r.rs:
                    if rd.eng != eng:
                        deps[id(rd)] = rd
        for w in writes:
            if w.w is not None:
                deps[id(w.w)] = w.w
            for rd in w.rs:
                deps[id(rd)] = rd
        if dma:
            k = self.dma_count[eng]
            self.dma_count[eng] += 1
            slot = k % self.n_dma_sems
            o.dsem = slot
            o.dval = 16 * (k // self.n_dma_sems + 1)
            prev = self.dma_last.get((eng, slot))
            if prev is not None:
                deps[id(prev)] = prev
            self.dma_last[(eng, slot)] = o
        for d in deps.values():
            if d is o:
                continue
            if (not d.dma) and d.eng == "pe" and eng == "pe" and not dma:
                continue
            o.deps.append(d)
            if not d.dma:
                d.need_sig = True
        for r in reads:
            if not dma:
                r.rs = [x for x in r.rs if x.dma or x.eng != eng]
            r.rs.append(o)
        for w in writes:
            w.w = o
            w.rs = []
        self.ops[eng].append(o)
        return o

    def emit(self, nc):
        for e in ENGS:
            n = 0
            for o in self.ops[e]:
                if o.need_sig:
                    n += 1
                    o.sig = n
        esem = {e: nc.alloc_semaphore(f"es_{e}") for e in ENGS}
        dsem = {}
        for e in ENGS:
            if self.dma_count[e]:
                dsem[e] = [nc.alloc_semaphore(f"ds_{e}_{i}") for i in range(self.n_dma_sems)]
        handles = {"pe": "tensor", "act": "scalar", "dve": "vector", "pool": "gpsimd", "sp": "sync"}
        prog = self
        all_sems = list(esem.values()) + [x for v in dsem.values() for x in v]
        for sm in all_sems:
            nc.gpsimd.sem_clear(sm)
        nc.all_engine_barrier()

        def run_engine(e, h):
            waited = {}
            for o in prog.ops[e]:
                need = {}
                for d in o.deps:
                    if d.dma:
                        key = ("d", d.eng, d.dsem)
                        val = d.dval
                    else:
                        key = ("e", d.eng)
                        val = d.sig
                    if waited.get(key, 0) >= val:
                        continue
                    if need.get(key, 0) < val:
                        need[key] = val
                for key, val in need.items():
                    waited[key] = val
                    sem = dsem[key[1]][key[2]] if key[0] == "d" else esem[key[1]]
                    h.wait_ge(sem, val)
                ins = o.fn(h)
                if o.dma:
                    ins.then_inc(dsem[e][o.dsem], 16)
                elif o.need_sig:
                    ins.then_inc(esem[e], 1)
            for o in prog.out_dmas:
                if o.eng == e and waited.get(("d", e, o.dsem), 0) < o.dval:
                    waited[("d", e, o.dsem)] = o.dval
                    h.wait_ge(dsem[e][o.dsem], o.dval)

        with nc.Block() as block:
            for e in ENGS:
                if not prog.ops[e]:
                    continue
                getattr(block, handles[e])(lambda h, e=e: run_engine(e, h))
        for sm in all_sems:
            nc.gpsimd.sem_clear(sm)
        nc.all_engine_barrier()


def carry_of(res_list):
    ops = {}
    for r in res_list:
        if r.w is not None:
            ops[id(r.w)] = r.w
        for x in r.rs:
            ops[id(x)] = x
    best = {}
    out = []
    for o in ops.values():
        if o.dma:
            out.append(o)
        else:
            b = best.get(o.eng)
            if b is None or o.idx > b.idx:
                best[o.eng] = o
    return out + list(best.values())


VEC_LAYOUT = {}
_c = 0
for _n, _w in [("ev_norm", 8), ("od_norm", 8), ("ffn_norm0", 8), ("ffn_norm1", 8),
               ("ple_norm0", 8), ("ple_norm1", 8), ("final_norm", 8),
               ("od_b_in", 16), ("od_dw_b", 8), ("od_ln_g", 8), ("od_ln_b", 8), ("od_b_out", 8),
               ("od_dw_w", 248), ("ffn_dw_w0", 132), ("ffn_dw_w1", 132),
               ("ffn_dw_b0", 44), ("ffn_dw_b1", 44), ("ev_b_conv_w", 12),
               ("ev_a_ln_g", 4), ("ev_a_ln_b", 4)]:
    VEC_LAYOUT[_n] = _c
    _c += _w
NVEC = _c


def _chan(v):
    v = np.asarray(v, np.float32).reshape(-1)
    return v.reshape(-1, 128).T


def _taps(w):
    w = np.asarray(w, np.float32)
    K, C = w.shape
    return w.reshape(K, C // 128, 128).transpose(2, 1, 0).reshape(128, (C // 128) * K)


def pack_vecs(inp):
    v = np.zeros((128, NVEC), np.float32)

    def put(name, arr):
        c = VEC_LAYOUT[name]
        v[:, c:c + arr.shape[1]] = arr

    put("ev_norm", _chan(inp["ev_norm"][0]))
    put("od_norm", _chan(inp["od_norm"][0]))
    for i in range(2):
        put(f"ffn_norm{i}", _chan(inp["ffn_norm"][i]))
        put(f"ple_norm{i}", _chan(inp["ple_norm"][i]))
        tw = _taps(inp["ffn_dw_w"][i]).reshape(128, 2, 22, 3)
        put(f"ffn_dw_w{i}", np.ascontiguousarray(tw.transpose(0, 2, 1, 3)).reshape(128, 132))
        put(f"ffn_dw_b{i}", _chan(inp["ffn_dw_b"][i]))
    put("final_norm", _chan(inp["final_norm"]))
    put("od_b_in", _chan(inp["od_b_in"][0]))
    put("od_dw_b", _chan(inp["od_dw_b"][0]))
    put("od_ln_g", _chan(inp["od_ln_g"][0]))
    put("od_ln_b", _chan(inp["od_ln_b"][0]))
    put("od_b_out", _chan(inp["od_b_out"][0]))
    put("od_dw_w", _taps(inp["od_dw_w"][0]))
    put("ev_b_conv_w", _taps(inp["ev_b_conv_w"][0]))
    put("ev_a_ln_g", _chan(inp["ev_a_ln_g"][0]))
    put("ev_a_ln_b", _chan(inp["ev_a_ln_b"][0]))
    return v


class Builder:
    def __init__(self, nstages=7, nseq=NSEQ):
        self.nstages = nstages
        self.nseq = nseq
        nc = bass.Bass("TRN2", target_bir_lowering=False)
        self.nc = nc

        def din(name, shape):
            return nc.dram_tensor(name, list(shape), F32, kind="ExternalInput").ap()

        self.xT = din("xT", [D, NSEQ * S])
        self.pT = din("pT", [2, 256, NSEQ * S])
        self.w_ev_in = din("w_ev_in", [D, 2560])
        self.w_ev_out = din("w_ev_out", [D, D])
        self.w_od_in = din("w_od_in", [D, 2048])
        self.w_od_out = din("w_od_out", [D, D])
        self.w_up = din("w_up", [2, D, 2 * DFF])
        self.w_down = din("w_down", [2, DFF, D])
        self.w_pp = din("w_pp", [2, 256, D])
        self.w_pg = din("w_pg", [2, D, D])
        self.vecs_d = din("vecs", [128, NVEC])
        self.wsT_d = din("wsT", [128, 512])
        self.bsbc_d = din("bsbc", [128, 512])
        self.oT = nc.dram_tensor("oT", [D, NSEQ * S], F32, kind="ExternalOutput").ap()

        self.r = nc.alloc_sbuf_tensor("r", [128, KC, S], F32)
        self.h = nc.alloc_sbuf_tensor("h", [128, KC, S], BF16)
        self.scr = nc.alloc_sbuf_tensor("scr", [128, 24576], BF16)
        self.wring = nc.alloc_sbuf_tensor("wring", [128, NSLOT, 2048], BF16)
        self.multi = nc.alloc_sbuf_tensor("multi", [128, 4160], BF16)
        self.vecs = nc.alloc_sbuf_tensor("vecs_sb", [128, NVEC], F32)
        self.ident = nc.alloc_sbuf_tensor("ident", [128, 128], BF16)
        self.identf = nc.alloc_sbuf_tensor("identf", [128, 128], F32)
        self.ones_m = nc.alloc_sbuf_tensor("ones_m", [128, 128], BF16)
        self.epsT = nc.alloc_sbuf_tensor("epsT", [128, 1], F32)
        self.dring = nc.alloc_sbuf_tensor("dring", [128, NDIAG, 128], BF16)
        self.tmp = nc.alloc_sbuf_tensor("tmp", [128, 8, 512], F32)
        self.Q = nc.alloc_sbuf_tensor("Q", [128, 4, 128], F32)
        self.wmT = nc.alloc_sbuf_tensor("wmT", [128, 4, 128], BF16)
        self.stat = nc.alloc_sbuf_tensor("stat", [128, 4, 16], F32)
        self.pst = nc.alloc_psum_tensor("pst", [128, 8, 512], F32)
        self.psb = [self.pst[:, i, :] for i in range(8)]

    def reset(self, plan):
        self.P = Prog()
        self.plan_in = plan
        self.plan_out = []
        self.wi = 0
        self.wissued = 0
        self.wcur = [-1] * NSLOT
        self.Rslot = [Res(f"ws{i}") for i in range(NSLOT)]
        self.psr = [Res(f"ps{i}", psum=True) for i in range(8)]
        self.psi = 0
        self.tmr = [Res(f"tm{i}") for i in range(8)]
        self.tmi = 0
        self.tm4 = 0
        self.tmn = 7
        self.wdepth = WDEPTH
        self.Rsqb = Res("sqb")
        self.tlong = (self.tmp[:, 7, :], self.tmr[7])
        self.Rr = [[Res(f"r{k}_{t}") for t in range(NT)] for k in range(KC)]
        self.Rh = [[Res(f"h{k}_{t}") for t in range(NT)] for k in range(KC)]
        self.Rconst = Res("const")
        self.Rvec = Res("vecs")
        self.Rd = [Res(f"dg{i}") for i in range(NDIAG)]
        self.di = 0
        self.Rscr = [Res("scr")]
        self.Rmulti = [Res("multi")]
        self.Rstat = [Res(f"st{i}") for i in range(4)]
        self.sti = 0

    def ps(self):
        i = self.psi
        self.psi = (i + 1) % 8
        return self.psb[i], self.psr[i]

    def ps4(self):
        i = 0 if self.psi <= 0 or self.psi > 4 else 4
        self.psi = (i + 4) % 8
        return self.pst[:, i:i + 4, :], self.psr[i:i + 4]

    def tmp4(self):
        i = self.tm4
        self.tm4 = 4 - i
        return self.tmp[:, i:i + 4, :], self.tmr[i:i + 4]

    def tmpf(self):
        i = self.tmi
        self.tmi = (i + 1) % self.tmn
        return self.tmp[:, i, :], self.tmr[i]

    def retile(self, which, n):
        old = self.Rscr if which == "scr" else self.Rmulti
        carry = carry_of(old)
        new = []
        for i in range(n):
            r = Res(f"{which}{i}")
            r.rs = list(carry)
            new.append(r)
        if which == "scr":
            self.Rscr = new
        else:
            self.Rmulti = new
        return new

    def _issue_w(self, j, src, viewfn):
        slot = j % NSLOT
        view = viewfn(self.wring[:, slot, :])
        self.wcur[slot] = j
        if isinstance(src, (list, tuple)):
            for (sap, subfn) in src:
                sv = subfn(view)
                self.P.op("pool", lambda e, sv=sv, sap=sap: e.dma_start(out=sv, in_=sap),
                          writes=[self.Rslot[slot]], dma=True)
            return
        self.P.op("pool", lambda e, view=view, src=src: e.dma_start(out=view, in_=src),
                  writes=[self.Rslot[slot]], dma=True)

    def wget(self, src, viewfn):
        i = self.wi
        self.wi += 1
        self.plan_out.append((src, viewfn))
        if self.plan_in is None:
            self._issue_w(i, src, viewfn)
            self.wissued = i + 1
        else:
            lim = min(i + self.wdepth, len(self.plan_in))
            for j in range(self.wissued, lim):
                self._issue_w(j, *self.plan_in[j])
            self.wissued = max(self.wissued, lim)
        slot = i % NSLOT
        assert self.wcur[slot] == i
        return viewfn(self.wring[:, slot, :]), self.Rslot[slot], (slot, i)

    def wcheck(self, tag):
        assert self.wcur[tag[0]] == tag[1], "weight slot overwritten before use"

    def mm(self, ps, lhsT, rhs, start, stop, reads, pres):
        self.P.op("pe", lambda e: e.matmul(ps, lhsT, rhs, start=start, stop=stop),
                  reads=reads, writes=[pres])

    def act(self, out, in_, func, reads, writes, bias=None, scale=None):
        kw = {}
        if bias is not None:
            kw["bias"] = bias
        if scale is not None:
            kw["scale"] = scale
        self.P.op("act", lambda e: e.activation(out=out, in_=in_, func=func, **kw),
                  reads=reads, writes=writes)

    def stt(self, out, in0, scalar, in1, op0, op1, reads, writes, eng="dve"):
        self.P.op(eng, lambda e: e.scalar_tensor_tensor(out=out, in0=in0, scalar=scalar, in1=in1,
                                                         op0=op0, op1=op1),
                  reads=reads, writes=writes)

    def tt(self, out, in0, in1, op, reads, writes, eng="dve"):
        self.P.op(eng, lambda e: e.tensor_tensor(out=out, in0=in0, in1=in1, op=op),
                  reads=reads, writes=writes)

    def ts(self, out, in0, s1, s2, op0, op1, reads, writes, eng="dve"):
        if s2 is None:
            self.P.op(eng, lambda e: e.tensor_scalar(out=out, in0=in0, scalar1=s1, scalar2=None, op0=op0),
                      reads=reads, writes=writes)
        else:
            self.P.op(eng, lambda e: e.tensor_scalar(out=out, in0=in0, scalar1=s1, scalar2=s2,
                                                     op0=op0, op1=op1),
                      reads=reads, writes=writes)

    def vcol(self, name, j=0):
        c = VEC_LAYOUT[name] + j
        return self.vecs[:, c:c + 1]

    def diag_group(self, col, n, slot0):
        dst = self.dring[:, slot0:slot0 + n, :]
        in0 = self.identf[:].unsqueeze(1).broadcast_to([128, n, 128])
        in1 = self.vecs[:, col:col + n].unsqueeze(2).broadcast_to([128, n, 128])
        res = [self.Rd[slot0 + i] for i in range(n)]
        self.P.op("dve", lambda e: e.tensor_tensor(out=dst, in0=in0, in1=in1, op=ALU.mult),
                  reads=[self.Rconst, self.Rvec], writes=res)
        return [(self.dring[:, slot0 + i, :], res[i]) for i in range(n)]

    def diag6(self, col, n=6):
        g = self.di
        self.di = (g + 1) % 5
        return self.diag_group(col, n, 6 * g)

    def setup(self):
        P = self.P
        nc = self.nc
        vecs, vecs_d = self.vecs, self.vecs_d
        P.op("sp", lambda e: e.dma_start(out=vecs[:], in_=vecs_d), writes=[self.Rvec], dma=True)
        identf, ident, ones_m, epsT = self.identf, self.ident, self.ones_m, self.epsT
        P.op("pool", lambda e: e.memset(identf[:], 1.0), writes=[self.Rconst])
        P.op("pool", lambda e: e.affine_select(out=identf[:], in_=identf[:], pattern=[[-1, 128]],
                                               compare_op=ALU.is_equal, fill=0.0, base=0,
                                               channel_multiplier=1),
             reads=[self.Rconst], writes=[self.Rconst])
        P.op("pool", lambda e: e.tensor_copy(out=ident[:], in_=identf[:]), reads=[self.Rconst],
             writes=[self.Rconst])
        P.op("pool", lambda e: e.memset(ones_m[:], 1.0 / 1024.0), writes=[self.Rconst])
        P.op("pool", lambda e: e.memset(epsT[:], EPS), writes=[self.Rconst])
        t0, r0 = self.tmpf()
        wsT_d = self.wsT_d
        P.op("sp", lambda e: e.dma_start(out=t0, in_=wsT_d), writes=[r0], dma=True)
        t03 = t0.rearrange("p (a b) -> p a b", b=128)
        P.op("pool", lambda e: e.affine_select(out=t03, in_=t03, pattern=[[0, 4], [1, 128]],
                                               compare_op=ALU.is_ge, fill=0.0, base=0,
                                               channel_multiplier=-1),
             reads=[r0], writes=[r0])
        wmT = self.wmT
        self.Rwm = Res("wmT")
        P.op("dve", lambda e: e.tensor_copy(out=wmT[:], in_=t03), reads=[r0], writes=[self.Rwm])
        t1, r1 = self.tmpf()
        bsbc_d = self.bsbc_d
        P.op("sp", lambda e: e.dma_start(out=t1, in_=bsbc_d), writes=[r1], dma=True)
        ps, pr = self.ps()
        wm2 = wmT[:].rearrange("p a b -> p (a b)")
        self.mm(ps[:], ones_m[:], wm2, True, True, [self.Rconst, self.Rwm], pr)
        t2, r2 = self.tmpf()
        self.RQ = Res("Q")
        for hd in range(4):
            sl = slice(hd * 128, (hd + 1) * 128)
            self.ts(t2[:, sl], ps[:, sl], self.vcol("ev_a_ln_b", hd), None, ALU.mult, None,
                    [pr, self.Rvec], [r2])
        Qf = self.Q[:].rearrange("p a b -> p (a b)")
        self.stt(Qf, t2, 1024.0, t1, ALU.mult, ALU.add, [r2, r1], [self.RQ])

    def load_x_tt(self, s, tt):
        r = self.r
        src = self.xT.rearrange("(k p) n -> p k n", p=128)
        a, b = s * S + tt * TT, s * S + (tt + 1) * TT
        dst = r[:, :, tt * TT:(tt + 1) * TT]
        self.P.op("sp", lambda e, dst=dst, sv=src[:, :, a:b]: e.dma_start(out=dst, in_=sv),
                  writes=[self.Rr[k][tt] for k in range(KC)], dma=True)

    def load_x(self, s):
        for tt in range(NT):
            self.load_x_tt(s, tt)

    def norm(self, gname):
        for tt in range(NT):
            self._norm_one(tt, gname)

    def out_proj_tt(self, wsrc, src_ap_fn, src_res_fn, bias_name=None, hook=None):
        r = self.r
        self.wdepth = 1
        ws_ = []
        for cp in range(4):
            if cp == 3:
                self.wdepth = 2
            ws_.append(self.wget(wsrc.rearrange("(k p) n -> p k n", p=128)[:, :, cp * 256:(cp + 1) * 256],
                                 lambda s_: s_.rearrange("p (k n) -> p k n", k=KC)))
        self.wdepth = WDEPTH
        for tt in range(NT):
            sl = slice(tt * TT, (tt + 1) * TT)
            for cp in range(4):
                wv, wr, tag = ws_[cp]
                for b in range(2):
                    c = 2 * cp + b
                    ps, pr = self.ps()
                    self.wcheck(tag)
                    for k in range(KC):
                        self.mm(ps[:], wv[:, k, b * 128:(b + 1) * 128], src_ap_fn(k, tt), k == 0, k == KC - 1,
                                [wr, src_res_fn(k, tt)], pr)
                    rr = self.Rr[c][tt]
                    if bias_name is None:
                        self.tt(r[:, c, sl], ps[:], r[:, c, sl], ALU.add, [pr, rr], [rr])
                    else:
                        self.stt(r[:, c, sl], ps[:], self.vcol(bias_name, c), r[:, c, sl], ALU.add, ALU.add,
                                 [pr, rr, self.Rvec], [rr])
            if hook is not None and tt >= 1:
                hook(tt - 1)
        if hook is not None:
            hook(NT - 1)

    def out_proj(self, wsrc, src_ap_fn, src_res_fn, bias_name=None, hook=None):
        r = self.r
        for cp in range(4):
            wv, wr, tag = self.wget(wsrc.rearrange("(k p) n -> p k n", p=128)[:, :, cp * 256:(cp + 1) * 256],
                                    lambda s: s.rearrange("p (k n) -> p k n", k=KC))
            for tt in range(NT):
                sl = slice(tt * TT, (tt + 1) * TT)
                for b in range(2):
                    c = 2 * cp + b
                    ps, pr = self.ps()
                    self.wcheck(tag)
                    for k in range(KC):
                        self.mm(ps[:], wv[:, k, b * 128:(b + 1) * 128], src_ap_fn(k, tt), k == 0, k == KC - 1,
                                [wr, src_res_fn(k, tt)], pr)
                    rr = self.Rr[c][tt]
                    if bias_name is None:
                        self.tt(r[:, c, sl], ps[:], r[:, c, sl], ALU.add, [pr, rr], [rr])
                    else:
                        self.stt(r[:, c, sl], ps[:], self.vcol(bias_name, c), r[:, c, sl], ALU.add, ALU.add,
                                 [pr, rr, self.Rvec], [rr])
                if cp == 3 and hook is not None and tt >= 1:
                    hook(tt - 1)
            if cp == 3 and hook is not None:
                hook(NT - 1)

    def ffn(self, i, s, hook=None):
        P = self.P
        h, r = self.h, self.r
        Rm = self.retile("multi", 7)
        zsb = self.multi[:, 0:2050]
        Rz = Rm[0:4]
        Rzh = Rm[4]
        P.op("dve", lambda e: e.memset(zsb[:, 0:2], 0.0), writes=[Rzh])
        hal = self.multi[:, 2056:2064].bitcast(F32)
        halo = [hal[:, 0:2], hal[:, 2:4]]
        Rhalo = Rm[5:7]
        bcol = VEC_LAYOUT[f"ffn_dw_b{i}"]
        A = self.scr[:, :].rearrange("p (j n) -> p j n", n=S)
        wup = self.w_up[i].rearrange("(k p) n -> p k n", p=128)
        wdn = self.w_down[i].rearrange("(j p) n -> p j n", p=128)
        dwc = VEC_LAYOUT[f"ffn_dw_w{i}"]
        for (j0, nb) in ((0, 12), (12, 10)):
            Rs = self.retile("scr", nb * NT)
            Ra = [[Rs[j * NT + t] for t in range(NT)] for j in range(nb)]
            tiles = []
            for pr_ in range(nb // 2):
                for b in range(2):
                    for tt in range(NT):
                        tiles.append((pr_, b, tt))
            state = {}

            def U(tile):
                pr_, b, tt = tile
                jl = 2 * pr_ + b
                jj = j0 + jl
                if b == 0 and tt == 0:
                    g0 = jj * 128
                    state["G"] = self.wget(wup[:, :, g0:g0 + 256], lambda s_: s_.rearrange("p (k n) -> p k n", k=KC))
                    state["U"] = self.wget(wup[:, :, DFF + g0:DFF + g0 + 256],
                                           lambda s_: s_.rearrange("p (k n) -> p k n", k=KC))
                if tt == 0:
                    state[("du", jl)] = self.diag6(dwc + jj * 6 + 3, 3)
                sl = slice(tt * TT, (tt + 1) * TT)
                zs = slice(2 + tt * TT, 2 + (tt + 1) * TT)
                wv, wr, tag = state["G"]
                self.wcheck(tag)
                ps, pr = self.ps()
                for k in range(KC):
                    self.mm(ps[:], wv[:, k, b * 128:(b + 1) * 128], h[:, k, sl], k == 0, k == KC - 1,
                            [wr, self.Rh[k][tt]], pr)
                w0, w1, w2 = [self.vecs[:, dwc + jj * 6 + t_:dwc + jj * 6 + t_ + 1] for t_ in range(3)]
                acc, racc = self.tmpf()
                self.act(acc, ps[:], AF.Identity, [pr, self.Rvec], [racc],
                         bias=self.vecs[:, bcol + jj:bcol + jj + 1], scale=w2)
                if tt < NT - 1:
                    P.op("dve", lambda e, o=halo[tt % 2], p_=ps: e.tensor_copy(out=o, in_=p_[:, TT - 2:TT]),
                         reads=[pr, racc], writes=[Rhalo[tt % 2]])
                self.stt(acc[:, 1:TT], ps[:, 0:TT - 1], w1, acc[:, 1:TT], ALU.mult, ALU.add,
                         [pr, racc, self.Rvec], [racc])
                self.stt(acc[:, 2:TT], ps[:, 0:TT - 2], w0, acc[:, 2:TT], ALU.mult, ALU.add,
                         [pr, racc, self.Rvec], [racc])
                if tt > 0:
                    hp, rhp = halo[(tt - 1) % 2], Rhalo[(tt - 1) % 2]
                    self.stt(acc[:, 0:1], hp[:, 1:2], w1, acc[:, 0:1], ALU.mult, ALU.add,
                             [rhp, racc, self.Rvec], [racc])
                    self.stt(acc[:, 0:2], hp[:, 0:2], w0, acc[:, 0:2], ALU.mult, ALU.add,
                             [rhp, racc, self.Rvec], [racc])
                state[("sg", jl, tt)] = (acc, racc)
                wv, wr, tag = state["U"]
                self.wcheck(tag)
                ps, pr = self.ps()
                for k in range(KC):
                    self.mm(ps[:], wv[:, k, b * 128:(b + 1) * 128], h[:, k, sl], k == 0, k == KC - 1,
                            [wr, self.Rh[k][tt]], pr)
                self.act(zsb[:, zs], ps[:], AF.Copy, [pr], [Rz[tt]])
                flush_silu()
                state["pend_silu"] = (acc, racc)

            def flush_silu():
                pnd = state.pop("pend_silu", None)
                if pnd is not None:
                    self.act(pnd[0], pnd[0], AF.Silu, [pnd[1]], [pnd[1]])

            def C(tile):
                pr_, b, tt = tile
                jl = 2 * pr_ + b
                jj = j0 + jl
                sl = slice(tt * TT, (tt + 1) * TT)
                ps, pr = self.ps()
                dgs = state[("du", jl)]
                rd = [Rz[tt]] + ([Rzh] if tt == 0 else [Rz[tt - 1]])
                for tap in range(3):
                    self.mm(ps[:], dgs[tap][0], zsb[:, tt * TT + tap: tt * TT + tap + TT],
                            tap == 0, tap == 2, rd + [dgs[tap][1]], pr)
                sg, rsg = state.pop(("sg", jl, tt))
                self.stt(A[:, jl, sl], ps[:], self.vecs[:, bcol + 22 + jj:bcol + 22 + jj + 1], sg,
                         ALU.add, ALU.mult, [pr, rsg, self.Rvec], [Ra[jl][tt]])

            pend = None
            for tile in tiles:
                U(tile)
                if pend is not None:
                    C(pend)
                pend = tile
            flush_silu()
            C(pend)
            if j0 != 0:
                self.load_p(i, s)
            dview = lambda s_, nb=nb: s_[:, 0:nb * 128].rearrange("p (j n) -> p j n", j=nb)
            for cp in range(4):
                ws_ = [self.wget(wdn[:, j0:j0 + nb, c * 128:(c + 1) * 128], dview) for c in (2 * cp, 2 * cp + 1)]
                for tt in range(NT):
                    sl = slice(tt * TT, (tt + 1) * TT)
                    for b in range(2):
                        c = 2 * cp + b
                        wv, wr, tag = ws_[b]
                        ps, pr = self.ps()
                        self.wcheck(tag)
                        for jl in range(nb):
                            self.mm(ps[:], wv[:, jl, :], A[:, jl, sl], jl == 0, jl == nb - 1,
                                    [wr, Ra[jl][tt]], pr)
                        rr = self.Rr[c][tt]
                        self.tt(r[:, c, sl], ps[:], r[:, c, sl], ALU.add, [pr, rr], [rr])
                    if j0 != 0 and cp == 3 and hook is not None and tt >= 1:
                        hook(tt - 1)
                if j0 != 0 and cp == 3 and hook is not None:
                    hook(NT - 1)

    def load_p(self, i, s):
        Rm = self.retile("multi", 1)
        self.Rp = Rm[0]
        pb = self.multi[:, 0:4096].rearrange("p (k n) -> p k n", k=2)
        src = self.pT[i].rearrange("(k p) n -> p k n", p=128)[:, :, s * S:(s + 1) * S]
        self.P.op("pool", lambda e: e.dma_start(out=pb, in_=src), writes=[self.Rp], dma=True)

    def ple(self, i, s, hook=None):
        r, h = self.r, self.h
        pb = self.multi[:, 0:4096].rearrange("p (k n) -> p k n", k=2)
        wpp = self.w_pp[i].rearrange("(k p) n -> p k n", p=128)
        wg = self.w_pg[i].rearrange("(k p) n -> p k n", p=128)
        for cp in range(4):
            wv, wr, tag = self.wget(wg[:, :, cp * 256:(cp + 1) * 256],
                                    lambda s_: s_.rearrange("p (k n) -> p k n", k=KC))
            wp, wpr, wptag = self.wget(wpp[:, :, cp * 256:(cp + 1) * 256],
                                       lambda s_: s_[:, 0:512].rearrange("p (k n) -> p k n", k=2))
            for tt in range(NT):
                sl = slice(tt * TT, (tt + 1) * TT)
                if cp == 3 and hook is not None and tt >= 1:
                    hook(tt - 1, 0)
                for b in range(2):
                    c = 2 * cp + b
                    self.wcheck(tag)
                    self.wcheck(wptag)
                    psg, prg = self.ps()
                    for k in range(KC):
                        self.mm(psg[:], wv[:, k, b * 128:(b + 1) * 128], h[:, k, sl], k == 0, k == KC - 1,
                                [wr, self.Rh[k][tt]], prg)
                    psp, prp = self.ps()
                    for k in range(2):
                        self.mm(psp[:], wp[:, k, b * 128:(b + 1) * 128], pb[:, k, sl], k == 0, k == 1,
                                [wpr, self.Rp], prp)
                    t, tr = self.tmpf()
                    self.act(t, psg[:], AF.Sigmoid, [prg], [tr])
                    self.tt(t, psp[:], t, ALU.mult, [prp, tr], [tr])
                    rr = self.Rr[c][tt]
                    self.tt(r[:, c, sl], t, r[:, c, sl], ALU.add, [tr, rr], [rr])
                if cp == 3 and hook is not None and tt >= 1:
                    hook(tt - 1, 1)
            if cp == 3 and hook is not None:
                hook(NT - 1)

    def mix0(self, s, hook=None, prenorm=False):
        P = self.P
        r, h = self.r, self.h
        Rs = self.retile("scr", 8 * NT + 16)
        Rmo = [[Rs[j * NT + t] for t in range(NT)] for j in range(8)]
        Rvn = [Rs[32 + ch] for ch in range(16)]
        MO = self.scr[:, 0:16384].rearrange("p (j n) -> p j n", n=S)
        VN = self.scr[:, 16384:24576].rearrange("p (c n) -> p c n", n=512)
        win = self.w_ev_in.rearrange("(k p) n -> p k n", p=128)
        kview = lambda s_: s_.rearrange("p (k n) -> p k n", k=KC)
        stat = self.stat[:].rearrange("p a b -> p (a b)")
        rst = self.Rstat[0]
        Rm = self.retile("multi", 6)
        s4 = self.multi[:, 2080:2080 + 2048].rearrange("p (c n) -> p c n", c=4)
        rs4 = [Rm[5]]
        v4 = self.tmp[:, 4:8, :]
        rv4 = self.tmr[4:8]
        self.tmn = 4
        self.tmi = 0
        xsb = self.multi[:, 0:2050]
        Rx, Rxh = Rm[0:4], Rm[4]
        P.op("dve", lambda e: e.memset(xsb[:, 0:2], 0.0), writes=[Rxh])
        cwc = VEC_LAYOUT["ev_b_conv_w"]

        def Bblock(kb, after):
            ch_ = self.wget([(win[:, :, 1536 + kb * 128:1536 + (kb + 1) * 128], lambda v: v[:, :, 0, :]),
                             (win[:, :, 2048 + kb * 128:2048 + (kb + 1) * 128], lambda v: v[:, :, 1, :])],
                            lambda s_: s_.rearrange("p (k g n) -> p k g n", k=KC, g=2))
            bbw = self.wget(win[:, :, 1024 + kb * 128:1024 + (kb + 1) * 128],
                            lambda s_: s_[:, 0:1024].rearrange("p (k n) -> p k n", k=KC))
            dgs = self.diag6(cwc + kb * 3, 3)
            for tt in range(NT):
                sl = slice(tt * TT, (tt + 1) * TT)
                self.wcheck(ch_[2])
                self.wcheck(bbw[2])
                pH, rH = self.ps()
                for k in range(KC):
                    self.mm(pH[:], ch_[0][:, k, 1, :], h[:, k, sl], k == 0, k == KC - 1, [ch_[1], self.Rh[k][tt]], rH)
                pC, rC = self.ps()
                for k in range(KC):
                    self.mm(pC[:], ch_[0][:, k, 0, :], h[:, k, sl], k == 0, k == KC - 1, [ch_[1], self.Rh[k][tt]], rC)
                pB, rB = self.ps()
                for k in range(KC):
                    self.mm(pB[:], bbw[0][:, k, :], h[:, k, sl], k == 0, k == KC - 1, [bbw[1], self.Rh[k][tt]], rB)
                hh, rhh = self.tmpf()
                self.act(hh, pH[:], AF.Copy, [rH], [rhh])
                self.tt(xsb[:, 2 + tt * TT:2 + (tt + 1) * TT], pC[:], hh, ALU.mult, [rC, rhh], [Rx[tt]])
                pV, rV = self.ps()
                rd = [Rx[tt]] + ([Rxh] if tt == 0 else [Rx[tt - 1]])
                for tap in range(3):
                    self.mm(pV[:], dgs[tap][0], xsb[:, tt * TT + tap:tt * TT + tap + TT], tap == 0, tap == 2,
                            rd + [dgs[tap][1]], rV)
                bb, rbb = self.tmpf()
                self.act(bb, pB[:], AF.Copy, [rB], [rbb])
                self.tt(MO[:, 4 + kb, sl], pV[:], bb, ALU.mult, [rV, rbb], [Rmo[4 + kb][tt]])
                after[tt]()

        def Vphases(tt):
            v16 = v4.rearrange("p c (a b) -> p (c a) b", b=128)
            s16 = s4.rearrange("p c (a b) -> p (c a) b", b=128)

            def ph0():
                va = self.wget(win[:, :, 512:768], kview)
                vb = self.wget(win[:, :, 768:1024], kview)
                P4, r4 = self.ps4()
                for q in range(4):
                    ch = tt * 4 + q
                    tok = slice(ch * 128, (ch + 1) * 128)
                    for half, wv in ((0, va), (1, vb)):
                        self.wcheck(wv[2])
                        for k in range(KC):
                            self.mm(P4[:, q, half * 256:(half + 1) * 256], h[:, k, tok], wv[0][:, k, :], k == 0,
                                    k == KC - 1, [wv[1], self.Rh[k][tt]], r4[q])
                self.act(v4, P4, AF.Gelu_apprx_tanh, list(r4), list(rv4))
                self.act(s4, v4, AF.Square, list(rv4), list(rs4))

            def ph1():
                P.op("dve", lambda e: e.tensor_reduce(out=stat[:, 0:16], in_=v16, axis=AX.X, op=ALU.add),
                     reads=list(rv4), writes=[rst])
                P.op("dve", lambda e: e.tensor_reduce(out=stat[:, 16:32], in_=s16, axis=AX.X, op=ALU.add),
                     reads=list(rs4), writes=[rst])

            def ph2():
                self.ts(stat[:, 32:48], stat[:, 0:16], 1.0 / 128.0, None, ALU.mult, None, [rst], [rst])
                self.tt(stat[:, 48:64], stat[:, 32:48], stat[:, 32:48], ALU.mult, [rst], [rst])
                self.stt(stat[:, 16:32], stat[:, 16:32], 1.0 / 128.0, stat[:, 48:64], ALU.mult, ALU.subtract,
                         [rst], [rst])
                self.ts(stat[:, 16:32], stat[:, 16:32], 0.0, None, ALU.max, None, [rst], [rst])
                self.act(stat[:, 16:32], stat[:, 16:32], AF.Ln, [rst, self.Rconst], [rst], bias=self.epsT[:, 0:1],
                         scale=1.0)
                self.act(stat[:, 16:32], stat[:, 16:32], AF.Exp, [rst], [rst], scale=-0.5)

            def ph3():
                mean_b = stat[:, 32:48].unsqueeze(2).broadcast_to([128, 16, 128])
                rstd_b = stat[:, 16:32].unsqueeze(2).broadcast_to([128, 16, 128])
                self.tt(v16, v16, mean_b, ALU.subtract, list(rv4) + [rst], list(rv4))
                vn16 = VN[:, tt * 4:(tt + 1) * 4, :].rearrange("p c (a b) -> p (c a) b", b=128)
                self.tt(vn16, v16, rstd_b, ALU.mult, list(rv4) + [rst], [Rvn[tt * 4 + q] for q in range(4)])

            return [ph0, ph1, ph2, ph3]

        self.wdepth = 2
        for kb in range(4):
            ph = Vphases(kb)
            if kb == 0 and prenorm:
                self._norm_one(0, "ev_norm")
                self._norm_one(1, "ev_norm")

                def mk(tt, f):
                    def g():
                        if tt + 2 < NT:
                            self._norm_one(tt + 2, "ev_norm")
                        f()
                    return g
                ph = [mk(tt, ph[tt]) for tt in range(NT)]
            Bblock(kb, ph)
        self.wdepth = WDEPTH
        self.tmn = 7
        Q, wmT = self.Q, self.wmT
        for hp in range(2):
            ua = self.wget(win[:, :, hp * 256:(hp + 1) * 256], kview)
            for b in range(2):
                hd = 2 * hp + b
                for tt in range(NT):
                    sl = slice(tt * TT, (tt + 1) * TT)
                    self.wcheck(ua[2])
                    ps, pr = self.ps()
                    for k in range(KC):
                        self.mm(ps[:], ua[0][:, k, b * 128:(b + 1) * 128], h[:, k, sl], k == 0, k == KC - 1,
                                [ua[1], self.Rh[k][tt]], pr)
                    u, ru = self.tmpf()
                    self.act(u, ps[:], AF.Gelu_apprx_tanh, [pr], [ru])
                    pm, prm = self.ps()
                    for q in range(4):
                        ch = tt * 4 + q
                        self.mm(pm[:, q * 128:(q + 1) * 128], VN[:, ch, hd * 128:(hd + 1) * 128], wmT[:, hd, :],
                                True, True, [Rvn[ch], self.Rwm], prm)
                    t, tr = self.tmpf()
                    qb = Q[:, hd, :].unsqueeze(1).broadcast_to([128, 4, 128])
                    self.stt(t.rearrange("p (a b) -> p a b", b=128), pm[:].rearrange("p (a b) -> p a b", b=128),
                             self.vcol("ev_a_ln_g", hd), qb, ALU.mult, ALU.add, [prm, self.RQ, self.Rvec], [tr])
                    self.tt(MO[:, hd, sl], t, u, ALU.mult, [tr, ru], [Rmo[hd][tt]])
        self.out_proj(self.w_ev_out, lambda k, tt: MO[:, k, tt * TT:(tt + 1) * TT], lambda k, tt: Rmo[k][tt],
                      hook=hook)

    def mix1(self, s, hook=None):
        P = self.P
        r, h = self.r, self.h
        Rs = self.retile("scr", 8 * NT)
        Ry = [[Rs[j * NT + t] for t in range(NT)] for j in range(8)]
        Y = self.scr[:, 0:16384].rearrange("p (j n) -> p j n", n=S)
        Rm = self.retile("multi", 10)
        XW = 2080
        xsb = [self.multi[:, 0:XW], self.multi[:, XW:2 * XW]]
        Rx = [[Rm[bb * 5 + t] for t in range(NT)] for bb in range(2)]
        Rxh = [Rm[4], Rm[9]]
        for bb in range(2):
            P.op("dve", lambda e, bb=bb: e.memset(xsb[bb][:, 0:30], 0.0), writes=[Rxh[bb]])
        win = self.w_od_in.rearrange("(k p) n -> p k n", p=128)
        dwc = VEC_LAYOUT["od_dw_w"]
        st = {}
        NPE = 24

        def IN(k):
            bb = k % 2
            ag = self.wget([(win[:, :, k * 128:(k + 1) * 128], lambda v: v[:, :, 0, :]),
                            (win[:, :, 1024 + k * 128:1024 + (k + 1) * 128], lambda v: v[:, :, 1, :])],
                           lambda s_: s_.rearrange("p (k g n) -> p k g n", k=KC, g=2))
            for tt in range(NT):
                sl = slice(tt * TT, (tt + 1) * TT)
                if tt == 2:
                    st[k] = self.diag_group(dwc + k * 31, NPE, 0)
                self.wcheck(ag[2])
                pA, rA = self.ps()
                for kc in range(KC):
                    self.mm(pA[:], ag[0][:, kc, 0, :], h[:, kc, sl], kc == 0, kc == KC - 1, [ag[1], self.Rh[kc][tt]], rA)
                pG, rG = self.ps()
                for kc in range(KC):
                    self.mm(pG[:], ag[0][:, kc, 1, :], h[:, kc, sl], kc == 0, kc == KC - 1, [ag[1], self.Rh[kc][tt]], rG)
                sg, rsg = self.tmpf()
                self.act(sg, pG[:], AF.Sigmoid, [rG, self.Rvec], [rsg], bias=self.vcol("od_b_in", 8 + k), scale=1.0)
                self.stt(xsb[bb][:, 30 + tt * TT:30 + (tt + 1) * TT], pA[:], self.vcol("od_b_in", k), sg,
                         ALU.add, ALU.mult, [rA, rsg, self.Rvec], [Rx[bb][tt]])

        def CONV(k, ln=None):
            bb = k % 2
            dgs = st[k]
            for tt in range(NT):
                if ln is not None and tt >= 2:
                    ln(tt - 2)
                sl = slice(tt * TT, (tt + 1) * TT)
                pY, rY = self.ps()
                rd = [Rx[bb][tt]] + ([Rxh[bb]] if tt == 0 else [Rx[bb][tt - 1]])
                for tap in range(NPE):
                    self.mm(pY[:], dgs[tap][0], xsb[bb][:, tt * TT + tap:tt * TT + tap + TT], tap == 0, tap == NPE - 1,
                            rd + [dgs[tap][1]], rY)
                acc, racc = self.tmpf()
                wcol = lambda j: self.vecs[:, dwc + k * 31 + j:dwc + k * 31 + j + 1]
                self.act(acc, xsb[bb][:, tt * TT + 30:tt * TT + 30 + TT], AF.Identity, rd + [self.Rvec], [racc],
                         scale=wcol(30))
                for j in range(NPE, 30):
                    self.stt(acc, xsb[bb][:, tt * TT + j:tt * TT + j + TT], wcol(j), acc, ALU.mult, ALU.add,
                             rd + [racc, self.Rvec], [racc])
                self.stt(Y[:, k, sl], pY[:], self.vcol("od_dw_b", k), acc, ALU.add, ALU.add,
                         [rY, racc, self.Rvec], [Ry[k][tt]])
            if ln is not None:
                ln(NT - 2)
                ln(NT - 1)

        lnst = {}
        longs = [self.tlong, (self.tmp[:, 6, :], self.tmr[6])]

        def LNa(tt):
            sl = slice(tt * TT, (tt + 1) * TT)
            yres = [Ry[k][tt] for k in range(KC)]
            hres = [self.Rh[k][tt] for k in range(KC)]
            p1, r1 = self.ps()
            for k in range(KC):
                self.mm(p1[:], self.ones_m[:], Y[:, k, sl], k == 0, k == KC - 1, [self.Rconst, yres[k]], r1)
            self.act(h[:, :, sl], Y[:, :, sl], AF.Square, yres, hres)
            p2, r2 = self.ps()
            for k in range(KC):
                self.mm(p2[:], self.ones_m[:], h[:, k, sl], k == 0, k == KC - 1, [self.Rconst, hres[k]], r2)
            mq, rmq = longs[tt % 2]
            self.act(mq, p1[:], AF.Square, [r1], [rmq])
            lnst[tt] = (p1, r1, mq, rmq, p2, r2)

        def LNa2(tt):
            p1, r1, mq, rmq, p2, r2 = lnst[tt]
            self.tt(mq, p2[:], mq, ALU.subtract, [r2, rmq], [rmq])
            self.ts(mq, mq, 0.0, None, ALU.max, None, [rmq], [rmq])
            self.act(mq, mq, AF.Ln, [rmq, self.Rconst], [rmq], bias=self.epsT[:, 0:1], scale=1.0)
            self.act(mq, mq, AF.Exp, [rmq], [rmq], scale=-0.5)

        def LNb(tt):
            sl = slice(tt * TT, (tt + 1) * TT)
            yres = [Ry[k][tt] for k in range(KC)]
            p1, r1, mq, rmq, _p2, _r2 = lnst.pop(tt)
            for k in range(KC):
                t, tr = self.tmpf()
                self.tt(t, Y[:, k, sl], p1[:], ALU.subtract, [yres[k], r1], [tr])
                self.tt(t, t, mq, ALU.mult, [tr, rmq], [tr])
                self.act(Y[:, k, sl], t, AF.Silu, [tr, self.Rvec], [yres[k]],
                         bias=self.vcol("od_ln_b", k), scale=self.vcol("od_ln_g", k))

        def LN(tt):
            LNa(tt)
            if tt >= 1:
                LNb(tt - 1)
            LNa2(tt)
            if tt == NT - 1:
                LNb(tt)

        self.tmn = 6
        self.tmi = 0
        for k in range(KC):
            IN(k)
            CONV(k, LN if k == KC - 1 else None)
        self.out_proj_tt(self.w_od_out, lambda k, tt: Y[:, k, tt * TT:(tt + 1) * TT], lambda k, tt: Ry[k][tt],
                         bias_name="od_b_out", hook=hook)
        self.tmn = 7

    def final_prep(self):
        Rs = self.retile("scr", 2)
        O = [self.scr[:, 0:8192].bitcast(F32).rearrange("p (k n) -> p k n", k=KC),
             self.scr[:, 8192:16384].bitcast(F32).rearrange("p (k n) -> p k n", k=KC)]
        self.fin = [(O[tt % 2], Rs[tt % 2]) for tt in range(NT)]

    def final_tt(self, s, tt, phase=None):
        dst = self.oT.rearrange("(k p) n -> p k n", p=128)
        self._norm_one(tt, "final_norm", self.fin[tt], phase=phase)
        if phase == 0:
            return
        a = s * S + tt * TT
        o = self.P.op("sp", lambda e, src=self.fin[tt][0], d=dst[:, :, a:a + TT]: e.dma_start(out=d, in_=src),
                      reads=[self.fin[tt][1]], dma=True)
        self.P.out_dmas.append(o)
        if s + 1 < self.nseq:
            self.load_x_tt(s + 1, tt)

    def final_raw(self, s):
        dst = self.oT.rearrange("(k p) n -> p k n", p=128)
        for tt in range(NT):
            a = s * S + tt * TT
            src = self.r[:, :, tt * TT:(tt + 1) * TT]
            o = self.P.op("sp", lambda e, src=src, d=dst[:, :, a:a + TT]: e.dma_start(out=d, in_=src),
                          reads=[self.Rr[k][tt] for k in range(KC)], dma=True)
            self.P.out_dmas.append(o)

    def _norm_one(self, tt, gname, out=None, phase=None):
        r, h = self.r, self.h
        sl = slice(tt * TT, (tt + 1) * TT)
        rres = [self.Rr[k][tt] for k in range(KC)]
        hres = [self.Rh[k][tt] for k in range(KC)]
        if phase in (None, 0):
            self.act(h[:, :, sl], r[:, :, sl], AF.Square, rres, hres)
        if phase == 0:
            return
        ps, pr = self.ps()
        for k in range(KC):
            self.mm(ps[:], self.ones_m[:], h[:, k, sl], k == 0, k == KC - 1, [self.Rconst, hres[k]], pr)
        t, tr = self.tmpf()
        self.act(t, ps[:], AF.Ln, [pr, self.Rconst], [tr], bias=self.epsT[:, 0:1], scale=1.0)
        self.act(t, t, AF.Exp, [tr], [tr], scale=-0.5)
        for k in range(KC):
            if out is None:
                self.stt(h[:, k, sl], r[:, k, sl], self.vcol(gname, k), t, ALU.mult, ALU.mult,
                         [rres[k], tr, self.Rvec], [hres[k]])
            else:
                oap, ores = out
                self.stt(oap[:, k, :], r[:, k, sl], self.vcol(gname, k), t, ALU.mult, ALU.mult,
                         [rres[k], tr, self.Rvec, hres[k]], [ores])

    def record(self, plan):
        self.reset(plan)
        self.setup()
        names = ["ev_norm", "ffn_norm0", "ple_norm0", "od_norm", "ffn_norm1", "ple_norm1"]
        stages = [lambda s, hk: self.mix0(s, hk, prenorm=True), lambda s, hk: self.ffn(0, s, hk), lambda s, hk: self.ple(0, s, hk),
                  lambda s, hk: self.mix1(s, hk), lambda s, hk: self.ffn(1, s, hk), lambda s, hk: self.ple(1, s, hk)]
        n = min(self.nstages, 6)
        for s in range(self.nseq):
            if s == 0 or self.nstages < 7:
                self.load_x(s)
            for i in range(n):
                if i + 1 < n:
                    hk = lambda tt, phase=None, g=names[i + 1]: self._norm_one(tt, g, phase=phase)
                elif self.nstages >= 7:
                    self.final_prep()
                    hk = lambda tt, phase=None, s=s: self.final_tt(s, tt, phase)
                else:
                    hk = None
                stages[i](s, hk)
            if self.nstages < 7:
                self.final_raw(s)
        return self.plan_out

    def build(self):
        plan = self.record(None)
        self.record(plan)
        self.P.emit(self.nc)
        return self.nc


def make_in_maps(inp):
    x = np.asarray(inp["x"], np.float32)
    p = np.asarray(inp["p"], np.float32)
    vecs = pack_vecs(inp)
    ws = np.asarray(inp["ev_a_ws"], np.float32)[0]
    wsT = np.ascontiguousarray(ws.transpose(2, 0, 1)).reshape(128, 512)
    bs = np.asarray(inp["ev_a_bs"], np.float32)[0].reshape(1, 512)
    bsbc = np.ascontiguousarray(np.broadcast_to(bs, (128, 512)))
    shared = {
        "w_ev_in": np.ascontiguousarray(np.asarray(inp["ev_w_in"], np.float32)[0]),
        "w_ev_out": np.ascontiguousarray(np.asarray(inp["ev_w_out"], np.float32)[0]),
        "w_od_in": np.ascontiguousarray(np.asarray(inp["od_w_in"], np.float32)[0]),
        "w_od_out": np.ascontiguousarray(np.asarray(inp["od_w_out"], np.float32)[0]),
        "w_up": np.ascontiguousarray(np.asarray(inp["ffn_w_up"], np.float32)),
        "w_down": np.ascontiguousarray(np.asarray(inp["ffn_w_down"], np.float32)),
        "w_pp": np.ascontiguousarray(np.asarray(inp["ple_w_p"], np.float32)),
        "w_pg": np.ascontiguousarray(np.asarray(inp["ple_w_g"], np.float32)),
        "vecs": vecs, "wsT": wsT, "bsbc": bsbc,
    }
    maps = []
    for c in range(NCORE):
        xs = x[NSEQ * c:NSEQ * (c + 1)]
        xT = np.ascontiguousarray(xs.reshape(NSEQ * S, D).T)
        ps_ = p[:, NSEQ * c:NSEQ * (c + 1)]
        pT = np.ascontiguousarray(ps_.reshape(2, NSEQ * S, 256).transpose(0, 2, 1))
        m = dict(shared)
        m["xT"] = xT
        m["pT"] = pT
        maps.append(m)
    return maps


_NC_CACHE = {}


def get_nc(nstages=7, nseq=NSEQ):
    key = (nstages, nseq)
    if key not in _NC_CACHE:
        _NC_CACHE[key] = Builder(nstages, nseq).build()
    return _NC_CACHE[key]


def kernel(**inputs):
    nc = get_nc()
    maps = make_in_maps(inputs)
    res = run_bass_kernel_spmd(nc, maps, core_ids=list(range(NCORE)))
    out = np.empty((NCORE * NSEQ, S, D), np.float32)
    for c in range(NCORE):
        oT = np.asarray(res.results[c]["oT"])
        out[NSEQ * c:NSEQ * (c + 1)] = oT.T.reshape(NSEQ, S, D)
    return out
```
